# Optimizing a Trainium2 kernel written in Bass

```python
import functools
import jax, jax.numpy as jnp
from jax import lax
import numpy as np

D_MODEL = 1024
BATCH = 2
SEQ = 8192
DEPTH = 2
DEC_BATCH = 32
DEC_SEQ = 4
PAST_LEN = 8192
PAGE_SIZE = 128

GLA_HEADS = 4
GLA_DK = 64
GLA_DV = 128
GLA_GATE_RANK = 16
GLA_GATE_TEMP = 16.0
FOX_HEADS = 4
FOX_DH = 128
Q_BLOCK = 128
FOX_GATE_BIAS = 6.0
CACHE_LOGF_BIAS = 8.0
HG_HEADS = 4
HG_DK = 128
HG_DV = 128
CHUNK = 64
FFN_HIDDEN = -(-8 * D_MODEL // (3 * 256)) * 256
EPS = 1e-6
IN_WIDTH = (2 * GLA_HEADS * GLA_DK + 2 * GLA_HEADS * GLA_DV + GLA_GATE_RANK
            + 3 * FOX_HEADS * FOX_DH + FOX_HEADS
            + 2 * HG_HEADS * HG_DK + 2 * HG_HEADS * HG_DV + 3 * D_MODEL)

kernel_name = 'hybrid_gla_fox_hgrn2_step'


def _split_points():
    widths = (GLA_HEADS * GLA_DK, GLA_HEADS * GLA_DK, GLA_HEADS * GLA_DV, GLA_GATE_RANK, GLA_HEADS * GLA_DV,
              FOX_HEADS * FOX_DH, FOX_HEADS * FOX_DH, FOX_HEADS * FOX_DH, FOX_HEADS,
              HG_HEADS * HG_DK, HG_HEADS * HG_DK, HG_HEADS * HG_DV, HG_HEADS * HG_DV,
              D_MODEL, D_MODEL, D_MODEL)
    return [int(c) for c in np.cumsum(widths)[:-1]]


def _rms_norm(x, g):
    xf = x.astype(jnp.float32)
    y = xf * lax.rsqrt(jnp.mean(xf * xf, axis=-1, keepdims=True) + EPS)
    return (y * g.astype(jnp.float32)).astype(x.dtype)


def _gated_linear_scan(q, k, v, log_a, s0):
    B, T, H, K = q.shape
    V = v.shape[-1]
    C = min(CHUNK, T)
    n = -(-T // C)
    pad = n * C - T

    def prep(a):
        a = jnp.pad(a.astype(jnp.float32), ((0, 0), (0, pad), (0, 0), (0, 0)))
        return a.reshape(B, n, C, H, a.shape[-1]).transpose(1, 0, 3, 2, 4)

    qs, ks, vs, ls = prep(q), prep(k), prep(v), prep(log_a)
    causal = jnp.tril(jnp.ones((C, C), dtype=bool))

    def step(S, inp):
        qc, kc, vc, lc = inp
        b = jnp.cumsum(lc, axis=2)
        diff = b[:, :, :, None, :] - b[:, :, None, :, :]
        decay = jnp.exp(jnp.where(causal[:, :, None], diff, -jnp.inf))
        A = jnp.einsum('bhtk,bhsk,bhtsk->bhts', qc, kc, decay)
        o = (jnp.einsum('bhts,bhsv->bhtv', A, vc)
             + jnp.einsum('bhtk,bhkv->bhtv', qc * jnp.exp(b), S))
        b_last = b[:, :, -1:, :]
        S = (jnp.exp(b_last[:, :, 0, :])[..., None] * S
             + jnp.einsum('bhsk,bhsv->bhkv', kc * jnp.exp(b_last - b), vc))
        return S, o

    S, o = lax.scan(step, s0.astype(jnp.float32), (qs, ks, vs, ls))
    o = o.transpose(1, 0, 3, 2, 4).reshape(B, n * C, H, V)[:, :T]
    return o.astype(q.dtype), S.astype(s0.dtype)


def _fox_prompt_attend(q, k, v, logf):
    B, T, H, Dh = q.shape
    nb = T // Q_BLOCK
    c = jnp.cumsum(logf.astype(jnp.float32), axis=1)
    ck = c.transpose(0, 2, 1)[:, :, None, :]
    kpos = jnp.arange(T)
    scale = Dh ** -0.5
    qb = q.reshape(B, nb, Q_BLOCK, H, Dh).transpose(1, 0, 2, 3, 4)
    cq = c.reshape(B, nb, Q_BLOCK, H).transpose(1, 0, 3, 2)

    def block(args):
        i, qi, cqi = args
        s = jnp.einsum('bqhd,bkhd->bhqk', qi, k, preferred_element_type=jnp.float32) * scale
        qpos = i * Q_BLOCK + jnp.arange(Q_BLOCK)
        logits = jnp.where(kpos[None, :] <= qpos[:, None], s + cqi[..., None] - ck, -jnp.inf)
        p = jax.nn.softmax(logits, axis=-1)
        return jnp.einsum('bhqk,bkhd->bqhd', p.astype(v.dtype), v)

    o = lax.map(block, (jnp.arange(nb), qb, cq))
    return o.transpose(1, 0, 2, 3, 4).reshape(B, T, H, Dh)


def _fox_sample_attend(q, k, v, logf, cache_k, cache_v, cache_lf, page_table):
    DB, Tn, H, Dh = q.shape
    kp = cache_k[page_table].reshape(DB, -1, H, Dh)
    vp = cache_v[page_table].reshape(DB, -1, H, Dh)
    lp = cache_lf[page_table].reshape(DB, -1, H).astype(jnp.float32)
    r = jnp.cumsum(lp[:, ::-1], axis=1)[:, ::-1] - lp
    cn = jnp.cumsum(logf.astype(jnp.float32), axis=1).transpose(0, 2, 1)
    bias_past = cn[..., None] + r.transpose(0, 2, 1)[:, :, None, :]
    causal = jnp.tril(jnp.ones((Tn, Tn), dtype=bool))
    bias_new = jnp.where(causal, cn[..., None] - cn[:, :, None, :], -jnp.inf)
    kk = jnp.concatenate([kp, k.astype(kp.dtype)], axis=1)
    vv = jnp.concatenate([vp, v.astype(vp.dtype)], axis=1)
    s = jnp.einsum('bqhd,bkhd->bhqk', q, kk, preferred_element_type=jnp.float32) * Dh ** -0.5
    p = jax.nn.softmax(s + jnp.concatenate([bias_past, bias_new], axis=-1), axis=-1)
    return jnp.einsum('bhqk,bkhd->bqhd', p.astype(vv.dtype), vv).astype(q.dtype)


def _mixer(xn, w_in, gla_wg2, gla_bg, gla_norm_g, fox_bf, hg_lb, hg_norm_g,
           w_branch_a, w_branch_b, w_branch_c, w_out, gla_s0, hg_s0, fox_attend):
    B, T, _ = xn.shape
    z = xn @ w_in
    (gq, gk, gv, glr, gr, fq, fk, fv, ff, hq, hf, hi, hg, ga, gb, gc) = jnp.split(z, _split_points(), axis=-1)
    heads = lambda a, h: a.reshape(B, T, h, -1)
    log_a = jax.nn.log_sigmoid((glr @ gla_wg2 + gla_bg).astype(jnp.float32)) / GLA_GATE_TEMP
    o_a, gla_s = _gated_linear_scan(heads(gq, GLA_HEADS) * GLA_DK ** -0.5, heads(gk, GLA_HEADS),
                                    heads(gv, GLA_HEADS), heads(log_a, GLA_HEADS), gla_s0)
    o_a = (_rms_norm(o_a, gla_norm_g) * jax.nn.silu(heads(gr, GLA_HEADS))).reshape(B, T, -1)
    logf = jax.nn.log_sigmoid((ff + fox_bf).astype(jnp.float32))
    fk_h, fv_h = heads(fk, FOX_HEADS), heads(fv, FOX_HEADS)
    o_b = fox_attend(heads(fq, FOX_HEADS), fk_h, fv_h, logf).reshape(B, T, -1)
    hf32 = hf.astype(jnp.float32)
    log_f = jnp.logaddexp(jnp.log(hg_lb), jnp.log1p(-hg_lb) + jax.nn.log_sigmoid(hf32))
    one_minus_f = (1.0 - hg_lb) * jax.nn.sigmoid(-hf32)
    o_c, hg_s = _gated_linear_scan(jax.nn.silu(heads(hq, HG_HEADS)), heads(one_minus_f, HG_HEADS),
                                   heads(hi, HG_HEADS), heads(log_f, HG_HEADS), hg_s0)
    o_c = (_rms_norm(o_c, hg_norm_g) * jax.nn.silu(heads(hg, HG_HEADS))).reshape(B, T, -1)
    merged = (jax.nn.sigmoid(ga) * (o_a @ w_branch_a)
              + jax.nn.sigmoid(gb) * (o_b @ w_branch_b)
              + jax.nn.sigmoid(gc) * (o_c @ w_branch_c))
    return merged @ w_out, gla_s, hg_s, fk_h, fv_h, logf


def _decoder(x, gla_s0, hg_s0, fox_attends, hg_lb, norm1_g, w_in, gla_wg2, gla_bg, gla_norm_g, fox_bf,
             hg_norm_g, w_branch_a, w_branch_b, w_branch_c, w_out, norm2_g, w_ffn_gate, w_ffn_up,
             w_ffn_down, final_norm_g):
    gla_st, hg_st, ks, vs, lfs = [], [], [], [], []
    for l in range(DEPTH):
        y, g_s, h_s, k, v, lf = _mixer(_rms_norm(x, norm1_g[l]), w_in[l], gla_wg2[l], gla_bg[l], gla_norm_g[l],
                                       fox_bf[l], hg_lb[l], hg_norm_g[l], w_branch_a[l], w_branch_b[l],
                                       w_branch_c[l], w_out[l], gla_s0[l], hg_s0[l], fox_attends[l])
        x = x + y
        h = _rms_norm(x, norm2_g[l])
        x = x + (jax.nn.silu(h @ w_ffn_gate[l]) * (h @ w_ffn_up[l])) @ w_ffn_down[l]
        gla_st.append(g_s)
        hg_st.append(h_s)
        ks.append(k)
        vs.append(v)
        lfs.append(lf)
    return (_rms_norm(x, final_norm_g), jnp.stack(gla_st), jnp.stack(hg_st),
            jnp.stack(ks), jnp.stack(vs), jnp.stack(lfs))


def setup_inputs(seed: int = 0) -> dict:
    key = jax.random.key(seed)
    ks = jax.random.split(key, 32)
    n_pages = PAST_LEN // PAGE_SIZE
    n_used = DEC_BATCH * n_pages
    n_pool = n_used + n_used // 4
    nrm = lambda k, shape, s: jax.random.normal(k, shape, jnp.float32) * s
    fox_w = FOX_HEADS * FOX_DH
    return {
        'x_prompt': nrm(ks[0], (BATCH, SEQ, D_MODEL), 1.0),
        'x_sample': nrm(ks[1], (DEC_BATCH, DEC_SEQ, D_MODEL), 1.0),
        'state_gla': nrm(ks[2], (DEPTH, DEC_BATCH, GLA_HEADS, GLA_DK, GLA_DV), 1.0),
        'cache_fox_k': nrm(ks[3], (DEPTH, n_pool, PAGE_SIZE, FOX_HEADS, FOX_DH), 1.0),
        'cache_fox_v': nrm(ks[4], (DEPTH, n_pool, PAGE_SIZE, FOX_HEADS, FOX_DH), 1.0),
        'cache_fox_logf': jax.nn.log_sigmoid(nrm(ks[5], (DEPTH, n_pool, PAGE_SIZE, FOX_HEADS), 0.5) + CACHE_LOGF_BIAS),
        'state_hgrn': nrm(ks[6], (DEPTH, DEC_BATCH, HG_HEADS, HG_DK, HG_DV), 1.0),
        'page_table': jax.random.permutation(ks[7], n_pool)[:n_used].reshape(DEC_BATCH, n_pages).astype(jnp.int32),
        'norm1_g': 1.0 + nrm(ks[8], (DEPTH, D_MODEL), 0.01),
        'w_in': nrm(ks[9], (DEPTH, D_MODEL, IN_WIDTH), D_MODEL ** -0.5),
        'gla_wg2': nrm(ks[10], (DEPTH, GLA_GATE_RANK, GLA_HEADS * GLA_DK), GLA_GATE_RANK ** -0.5),
        'gla_bg': nrm(ks[11], (DEPTH, GLA_HEADS * GLA_DK), 0.1),
        'gla_norm_g': 1.0 + nrm(ks[12], (DEPTH, GLA_DV), 0.01),
        'fox_bf': FOX_GATE_BIAS + nrm(ks[13], (DEPTH, FOX_HEADS), 0.1),
        'hg_lb_logits': nrm(ks[14], (DEPTH, HG_HEADS * HG_DK), 0.1),
        'hg_norm_g': 1.0 + nrm(ks[15], (DEPTH, HG_DV), 0.01),
        'w_branch_a': nrm(ks[16], (DEPTH, GLA_HEADS * GLA_DV, D_MODEL), (GLA_HEADS * GLA_DV) ** -0.5),
        'w_branch_b': nrm(ks[17], (DEPTH, fox_w, D_MODEL), fox_w ** -0.5),
        'w_branch_c': nrm(ks[18], (DEPTH, HG_HEADS * HG_DV, D_MODEL), (HG_HEADS * HG_DV) ** -0.5),
        'w_out': nrm(ks[19], (DEPTH, D_MODEL, D_MODEL), D_MODEL ** -0.5),
        'norm2_g': 1.0 + nrm(ks[20], (DEPTH, D_MODEL), 0.01),
        'w_ffn_gate': nrm(ks[21], (DEPTH, D_MODEL, FFN_HIDDEN), D_MODEL ** -0.5),
        'w_ffn_up': nrm(ks[22], (DEPTH, D_MODEL, FFN_HIDDEN), D_MODEL ** -0.5),
        'w_ffn_down': nrm(ks[23], (DEPTH, FFN_HIDDEN, D_MODEL), FFN_HIDDEN ** -0.5),
        'final_norm_g': 1.0 + nrm(ks[24], (D_MODEL,), 0.01),
    }


def reference(x_prompt, x_sample, state_gla, cache_fox_k, cache_fox_v, cache_fox_logf, state_hgrn, page_table,
              norm1_g, w_in, gla_wg2, gla_bg, gla_norm_g, fox_bf, hg_lb_logits, hg_norm_g,
              w_branch_a, w_branch_b, w_branch_c, w_out, norm2_g, w_ffn_gate, w_ffn_up, w_ffn_down,
              final_norm_g):
    lb_cum = jnp.cumsum(jax.nn.softmax(hg_lb_logits.astype(jnp.float32), axis=0), axis=0)
    hg_lb = lb_cum - lb_cum[:1]
    b_p = x_prompt.shape[0]
    gla0 = jnp.zeros((DEPTH, b_p, GLA_HEADS, GLA_DK, GLA_DV), x_prompt.dtype)
    hg0 = jnp.zeros((DEPTH, b_p, HG_HEADS, HG_DK, HG_DV), x_prompt.dtype)
    y_p, gla_p, hg_p, k_p, v_p, lf_p = _decoder(
        x_prompt, gla0, hg0, [_fox_prompt_attend] * DEPTH, hg_lb, norm1_g, w_in, gla_wg2, gla_bg, gla_norm_g,
        fox_bf, hg_norm_g, w_branch_a, w_branch_b, w_branch_c, w_out, norm2_g, w_ffn_gate, w_ffn_up,
        w_ffn_down, final_norm_g)
    sample_attends = [functools.partial(_fox_sample_attend, cache_k=cache_fox_k[l], cache_v=cache_fox_v[l],
                                        cache_lf=cache_fox_logf[l], page_table=page_table)
                      for l in range(DEPTH)]
    y_s, gla_s, hg_s, k_s, v_s, lf_s = _decoder(
        x_sample, state_gla, state_hgrn, sample_attends, hg_lb, norm1_g, w_in, gla_wg2, gla_bg, gla_norm_g,
        fox_bf, hg_norm_g, w_branch_a, w_branch_b, w_branch_c, w_out, norm2_g, w_ffn_gate, w_ffn_up,
        w_ffn_down, final_norm_g)
    return (y_p, y_s, gla_p, gla_s, k_p, v_p, lf_p, k_s, v_s, lf_s, hg_p, hg_s)
```

```python
import numpy as np
from contextlib import ExitStack
import concourse.bass as bass
import concourse.mybir as mybir
from concourse.bass_utils import run_bass_kernel_spmd

F32 = mybir.dt.float32
BF16 = mybir.dt.bfloat16
I32 = mybir.dt.int32
AF = mybir.ActivationFunctionType
ALU = mybir.AluOpType
AX = mybir.AxisListType

D = 1024
SEQ = 8192
NTOK = 2064
NPOOL = 2560
EPS = 1e-6
FH = 2816
SAME_ENGINE_SYNC = True
import os
STAGE = int(os.environ.get('MK_STAGE', '99'))
SMALL_CACHE = os.environ.get('MK_SMALLCACHE', '0') == '1'
DEBUG = os.environ.get('MK_DEBUG', '0') == '1'

FG = [("gq", 0, 64), ("gk", 64, 64), ("gr", 128, 128), ("glr", 256, 16), ("fq", 272, 128),
      ("fk", 400, 128), ("hq", 528, 128), ("hf", 656, 128), ("hg", 784, 128)]
NF = 912
NT_ = 705
TA = 448

C_ID = 0
C_CAT2 = 128
C_SUF2 = 386
C_MASK2 = 514
C_CAT4 = 642
C_SUF4 = 651
C_MASK4 = 655
C_U128 = 659
C_INC128 = 787
C_WG = 915
C_PM32 = 931
C_ONES = 932
C_MASKD = 1060
C_EPS = 1060 + 2048
C_LN8 = C_EPS + 1
NCONST = C_EPS + 4


def make_consts():
    c = np.zeros((128, NCONST), np.float32)
    p = np.arange(128)
    c[:, C_ID:C_ID + 128] = np.eye(128)
    s, t = p[:, None], p[None, :]
    same = (s // 64) == (t // 64)
    inc2 = (same & (s <= t)).astype(np.float32)
    mid = (t // 64) * 64 + 31
    rel2 = inc2 - (same & (s <= mid)).astype(np.float32)
    last2 = np.zeros((128, 2), np.float32)
    last2[:64, 0] = 1
    last2[64:, 1] = 1
    c[:, C_CAT2:C_CAT2 + 128] = inc2
    c[:, C_CAT2 + 128:C_CAT2 + 256] = rel2
    c[:, C_CAT2 + 256:C_CAT2 + 258] = last2
    c[:, C_SUF2:C_SUF2 + 128] = (same & (s > t)).astype(np.float32)
    c[:, C_MASK2:C_MASK2 + 128] = inc2
    s4, t4 = np.arange(4)[:, None], np.arange(4)[None, :]
    inc4 = (s4 <= t4).astype(np.float32)
    rel4 = inc4 - (s4 <= 1).astype(np.float32)
    c[:4, C_CAT4:C_CAT4 + 4] = inc4
    c[:4, C_CAT4 + 4:C_CAT4 + 8] = rel4
    c[:4, C_CAT4 + 8] = 1
    c[:4, C_SUF4:C_SUF4 + 4] = (s4 > t4).astype(np.float32)
    c[:4, C_MASK4:C_MASK4 + 4] = inc4
    c[:, C_U128:C_U128 + 128] = (s > t).astype(np.float32)
    c[:, C_INC128:C_INC128 + 128] = (s <= t).astype(np.float32)
    g = np.arange(16)
    c[:16, C_WG:C_WG + 16] = (g[:, None] > g[None, :]).astype(np.float32)
    c[:, C_PM32] = p % 32
    c[:, C_ONES:C_ONES + 128] = 1.0
    c[:, C_EPS] = EPS
    c[:, C_LN8] = np.log(0.125)
    f = np.arange(512)[None, :]
    for r in range(4):
        c[:, C_MASKD + r * 512:C_MASKD + (r + 1) * 512] = ((128 * r + p[:, None]) <= f).astype(np.float32)
    return c


class Res:
    __slots__ = ("name", "w", "rs")

    def __init__(self, name):
        self.name = name
        self.w = None
        self.rs = {}


class Eng:
    def __init__(self, name, handle, sem, in_order):
        self.name, self.h, self.sem, self.cnt = name, handle, sem, 0
        self.seen = {}
        self.in_order = in_order
        self.dsems = []
        self.dvals = []
        self.dnext = 0


class Ctx:
    def __init__(self, nc, es):
        self.nc = nc
        self.es = es
        self.engs = {}
        self.uid = 0
        self.all_dma = []

    def add_engine(self, name, handle, in_order=False, ndma=0):
        sem = self.es.enter_context(self.nc.semaphore("s_" + name))
        e = Eng(name, handle, sem, in_order)
        for i in range(ndma):
            e.dsems.append(self.es.enter_context(self.nc.semaphore("d_%s_%d" % (name, i))))
            e.dvals.append(0)
        self.engs[name] = e
        return e

    def _wait(self, e, tok):
        sem, val = tok
        k = id(sem)
        if e.seen.get(k, 0) >= val:
            return
        e.h.wait_ge(sem, val)
        e.seen[k] = val

    def _deps(self, e, reads, writes):
        deps = {}

        def add(tok):
            k = id(tok[0])
            if k not in deps or deps[k][1] < tok[1]:
                deps[k] = tok
        for r in reads:
            if r.w is not None:
                add(r.w)
        for w in writes:
            if w.w is not None:
                add(w.w)
            for tok in w.rs.values():
                add(tok)
        for k, tok in deps.items():
            if tok[0] is e.sem and (e.in_order or not SAME_ENGINE_SYNC):
                continue
            self._wait(e, tok)

    def _record(self, tok, reads, writes):
        k = id(tok[0])
        for r in reads:
            old = r.rs.get(k)
            if old is None or old[1] < tok[1]:
                r.rs[k] = tok
        for w in writes:
            w.w = tok
            w.rs = {}

    def op(self, en, fn, reads=(), writes=()):
        e = self.engs[en]
        self._deps(e, reads, writes)
        inst = fn(e.h)
        e.cnt += 1
        inst.then_inc(e.sem, 1)
        tok = (e.sem, e.cnt)
        e.seen[id(e.sem)] = e.cnt if e.in_order else e.seen.get(id(e.sem), 0)
        self._record(tok, reads, writes)
        return tok

    def dma(self, en, fn, reads=(), writes=()):
        e = self.engs[en]
        self._deps(e, reads, writes)
        i = e.dnext
        e.dnext = (e.dnext + 1) % len(e.dsems)
        sem = e.dsems[i]
        if e.dvals[i] > 0:
            self._wait(e, (sem, e.dvals[i]))
        inst = fn(e.h)
        e.dvals[i] += 16
        inst.then_inc(sem, 16)
        tok = (sem, e.dvals[i])
        self._record(tok, reads, writes)
        return tok

    def barrier(self):
        toks = []
        for e in self.engs.values():
            if e.cnt > 0:
                toks.append((e.sem, e.cnt))
            for s, v in zip(e.dsems, e.dvals):
                if v > 0:
                    toks.append((s, v))
        for e in self.engs.values():
            for tok in toks:
                if tok[0] is e.sem:
                    continue
                self._wait(e, tok)

    def final_wait(self):
        e = self.engs["sp"]
        for o in self.engs.values():
            if o.cnt > 0 and o is not e:
                self._wait(e, (o.sem, o.cnt))
            for s, v in zip(o.dsems, o.dvals):
                if v > 0:
                    self._wait(e, (s, v))


class Pool:
    def __init__(self, cx, es, name, shape, dtype, bufs, psum=False):
        self.tiles = []
        for i in range(bufs):
            cx.uid += 1
            nm = "%s_%d_%d" % (name, i, cx.uid)
            if psum:
                t = es.enter_context(cx.nc.psum_tensor(nm, shape, dtype))
            else:
                t = es.enter_context(cx.nc.sbuf_tensor(nm, shape, dtype))
            self.tiles.append((t, Res(nm)))
        self.i = 0

    def get(self):
        t = self.tiles[self.i]
        self.i = (self.i + 1) % len(self.tiles)
        return t


class SubPool:
    def __init__(self, tiles):
        self.tiles = list(tiles)
        self.i = 0

    def get(self):
        t = self.tiles[self.i]
        self.i = (self.i + 1) % len(self.tiles)
        return t


def single(cx, es, name, shape, dtype):
    cx.uid += 1
    name = "%s_%d" % (name, cx.uid)
    t = es.enter_context(cx.nc.sbuf_tensor(name, shape, dtype))
    return t, Res(name)


def build_program():
    nc = bass.Bass("TRN2", target_bir_lowering=False)

    def din(name, shape, dt=F32):
        return nc.dram_tensor(name, shape, dt, kind="ExternalInput").ap()

    def dout(name, shape, dt=F32):
        return nc.dram_tensor(name, shape, dt, kind="ExternalOutput").ap()

    def dint(name, shape, dt):
        return nc.dram_tensor(name, shape, dt, kind="Internal").ap()

    xp = din("xp", [2048, D])
    xs = din("xs", [16, D])
    consts_d = din("consts", [128, NCONST])
    wF_d = din("wF", [2, D, NF])
    wT_d = din("wT", [2, D, NT_])
    wG_d = din("wG", [2, D, 3 * D])
    wg2_d = din("wg2", [2, 16, 64])
    bg_d = din("bg", [2, 64])
    gng_d = din("gng", [2, 128])
    hng_d = din("hng", [2, 128])
    fbf_d = din("fbf", [2, 1])
    hlb_d = din("hlb", [2, 128])
    wba_d = din("wba", [2, 512, D])
    wbb_d = din("wbb", [2, 512, D])
    wbc_d = din("wbc", [2, 512, D])
    wo_d = din("wo", [2, D, D])
    n1_d = din("n1", [2, D])
    n2_d = din("n2", [2, D])
    nf_d = din("nf", [D])
    wfg_d = din("wfg", [2, D, FH])
    wfu_d = din("wfu", [2, D, FH])
    wfd_d = din("wfd", [2, FH, D])
    sgla_d = din("sgla", [2, 16, 64, 128])
    shg_d = din("shg", [2, 16, 128, 128])
    NROWS = 64 if SMALL_CACHE else 2 * NPOOL * 32
    ck_d = din("ck", [NROWS, 512])
    cv_d = din("cv", [NROWS, 512])
    clf_d = din("clf", [NROWS, 4])
    pt_d = din("pt", [16, 64], I32)
    ridx_d = din("ridx", [128, 24], I32)

    y_p = dout("y_p", [2048, D])
    y_s = dout("y_s", [16, D])
    gla_p = dout("gla_p", [2, 64, 128])
    gla_s = dout("gla_s", [2, 16, 64, 128])
    fk_p = dout("fk_p", [2, SEQ, 128])
    fv_p = dout("fv_p", [2, SEQ, 128])
    lf_p = dout("lf_p", [2, 64, 128])
    fk_s = dout("fk_s", [2, 16, 4, 128])
    fv_s = dout("fv_s", [2, 16, 4, 128])
    lf_s = dout("lf_s", [2, 16, 4])
    hg_p = dout("hg_p", [2, 128, 128])
    hg_s = dout("hg_s", [2, 16, 128, 128])

    a1 = dint("a1", [D, NTOK], BF16)
    b1 = dint("b1", [8, 512, NTOK], BF16)
    a2 = dint("a2", [4 * 384, NTOK], BF16)
    b2 = dint("b2", [12, 512, NTOK], BF16)
    xres = dout("xres", [D, NTOK]) if DEBUG else dint("xres", [D, NTOK], F32)
    dbg_o = dout("dbg_o", [128, 12, NTOK], BF16) if DEBUG else None
    dbg_m = dout("dbg_m", [128, 8, NTOK], BF16) if DEBUG else None
    dbg_raw = dout("dbg_raw", [128, 512]) if DEBUG else None
    dbg_b1 = dout("dbg_b1", [128, 4, NTOK], BF16) if DEBUG else None
    dbg_a1 = dout("dbg_a1", [128, NTOK], BF16) if DEBUG else None
    dbg_xn = dout("dbg_xn", [128, 8, 512], BF16) if DEBUG else None
    dbg_wt = dout("dbg_wt", [128, 8, NT_], BF16) if DEBUG else None
    dbg_zt = dout("dbg_zt", [128, 4, NT_]) if DEBUG else None
    dbg_gate = dout("dbg_gate", [128, 512]) if DEBUG else None
    R_a1, R_b1, R_a2, R_b2, R_xres = Res("a1"), Res("b1"), Res("a2"), Res("b2"), Res("xres")
    R_out = Res("outs")
    R_in = Res("ins")
    groups = [[0, 1, 2, 3], [4, 5, 6, 7]]

    with ExitStack() as es:
        cx = Ctx(nc, es)
        es.enter_context(nc.allow_non_contiguous_dma(reason="small params / strided outputs"))
        block = es.enter_context(nc.Block())
        cx.add_engine("pe", nc.tensor, in_order=True)
        cx.add_engine("act", nc.scalar)
        cx.add_engine("dve", nc.vector)
        cx.add_engine("pool", nc.gpsimd, ndma=8)
        cx.add_engine("sp", nc.sync, ndma=12)
        cc_cnt = [0]

        PS = Pool(cx, es, "ps", [128, 512], F32, int(os.environ.get("MK_NPS", "6")), psum=True)
        PSA = Pool(cx, es, "psa", [128, 512], F32, 2, psum=True) if os.environ.get("MK_NPS", "6") == "6" else PS
        CONST, R_const = single(cx, es, "CONST", [128, NCONST], F32)
        ONESB, R_onesb = single(cx, es, "ONESB", [128, 128], BF16)
        MASKDB, R_maskdb = single(cx, es, "MASKDB", [128, 2048], BF16)

        op, dma = cx.op, cx.dma

        def cst(c0, n, rows=128):
            return CONST[0:rows, c0:c0 + n]

        ident = cst(C_ID, 128)
        dma("sp", lambda h: h.dma_start(out=CONST[:, :], in_=consts_d), [R_in], [R_const])
        op("dve", lambda h: h.tensor_copy(ONESB[:, :], cst(C_ONES, 128)), [R_const], [R_onesb])
        op("dve", lambda h: h.tensor_copy(MASKDB[:, :], cst(C_MASKD, 2048)), [R_const], [R_maskdb])

        def mm(out, lhsT, rhs, reads, wres, start=True, stop=True):
            return op("pe", lambda h: h.matmul(out, lhsT, rhs, start=start, stop=stop), reads, [wres])

        def tp(out, in_, rows, reads, wres):
            return op("pe", lambda h: h.transpose(out, in_, CONST[0:rows, C_ID:C_ID + rows]), list(reads) + [R_const], [wres])

        def act(out, in_, func, reads, writes, bias=0.0, scale=1.0):
            if isinstance(bias, float):
                assert bias in (0.0, 1.0), bias
            else:
                reads = list(reads)
            return op("act", lambda h: h.activation(out, in_, func, bias=bias, scale=scale), reads, writes)

        def tt(en, out, in0, in1, alu, reads, writes):
            return op(en, lambda h: h.tensor_tensor(out, in0, in1, alu), reads, writes)

        def ts(en, out, in0, s1, op0, reads, writes, s2=None, op1=None):
            if op1 is None:
                return op(en, lambda h: h.tensor_scalar(out, in0, s1, None, op0), reads, writes)
            return op(en, lambda h: h.tensor_scalar(out, in0, s1, s2, op0, op1), reads, writes)

        def stt(en, out, in0, sc, in1, op0, op1, reads, writes):
            return op(en, lambda h: h.scalar_tensor_tensor(out, in0, sc, in1, op0, op1), reads, writes)

        def cp(en, out, in_, reads, writes):
            if en == "act":
                return op("act", lambda h: h.copy(out, in_), reads, writes)
            return op(en, lambda h: h.tensor_copy(out, in_), reads, writes)

        def recip(out, in_, reads, writes):
            return op("dve", lambda h: h.reciprocal(out, in_), reads, writes)

        def sigmoid_inplace(t_ap, src_ap, reads, res, neg=False):
            act(t_ap, src_ap, AF.Exp, reads, [res], scale=(1.0 if neg else -1.0))
            act(t_ap, t_ap, AF.Ln, [res], [res], bias=1.0)
            act(t_ap, t_ap, AF.Exp, [res], [res], scale=-1.0)

        def allgather(src, dst, rsrc, rdst, nchunks):
            e = cx.engs["pool"]
            cx._deps(e, [rsrc], [rdst])
            tok = None
            for k in range(nchunks):
                sem = es.enter_context(nc.semaphore("cc%d" % cc_cnt[0]))
                cc_cnt[0] += 1
                inst = e.h.collective_compute("AllGather", ALU.bypass, replica_groups=groups,
                                              ins=[src[k * 128:(k + 1) * 128, :]], outs=[dst[k]])
                inst.then_inc(sem, 1)
                tok = (sem, 1)
                cx._wait(e, tok)
            cx._record(tok, [rsrc], [rdst])

        halves = [(0, 1032, [(0, 512), (512, 512), (1024, 8)]), (1032, 1032, [(1032, 512), (1544, 504), (2048, 16)])]

        PRM = Pool(cx, es, "prm", [128, 64], F32, 1)
        prm, R_prm = PRM.get()
        LBB, R_lbb = single(cx, es, "LBB", [128, 256], F32)
        BGB, R_bgb = single(cx, es, "BGB", [128, 64], F32)
        WG2, R_wg2 = single(cx, es, "WG2", [16, 64], F32)

        def load_layer_params(l):
            dma("sp", lambda h: h.dma_start(out=prm[:, 0:8], in_=n1_d[l].rearrange("(c p) -> p c", p=128)), [R_in], [R_prm])
            dma("sp", lambda h: h.dma_start(out=prm[:, 8:16], in_=n2_d[l].rearrange("(c p) -> p c", p=128)), [R_in], [R_prm])
            nxt = n1_d[l + 1] if l == 0 else nf_d
            dma("sp", lambda h: h.dma_start(out=prm[:, 16:24], in_=nxt.rearrange("(c p) -> p c", p=128)), [R_in], [R_prm])
            dma("sp", lambda h: h.dma_start(out=prm[:, 24:25], in_=gng_d[l].rearrange("(p o) -> p o", o=1)), [R_in], [R_prm])
            dma("sp", lambda h: h.dma_start(out=prm[:, 25:26], in_=hng_d[l].rearrange("(p o) -> p o", o=1)), [R_in], [R_prm])
            dma("sp", lambda h: h.dma_start(out=prm[:, 26:27], in_=fbf_d[l].partition_broadcast(128)), [R_in], [R_prm])
            ts("dve", prm[:, 26:27], prm[:, 26:27], -1.0, ALU.mult, [R_prm], [R_prm])
            if l == 0:
                op("dve", lambda h: h.memset(prm[:, 27:28], 0.0), [], [R_prm])
            else:
                dma("sp", lambda h: h.dma_start(out=prm[:, 30:31], in_=hlb_d[0].rearrange("(p o) -> p o", o=1)), [R_in], [R_prm])
                dma("sp", lambda h: h.dma_start(out=prm[:, 31:32], in_=hlb_d[1].rearrange("(p o) -> p o", o=1)), [R_in], [R_prm])
                tt("dve", prm[:, 27:28], prm[:, 31:32], prm[:, 30:31], ALU.subtract, [R_prm], [R_prm])
                sigmoid_inplace(prm[:, 27:28], prm[:, 27:28], [R_prm], R_prm)
            ts("dve", prm[:, 28:29], prm[:, 27:28], -1.0, ALU.mult, [R_prm], [R_prm], s2=1.0, op1=ALU.add)
            dg_t, dg_r = TMPF.get()
            for j, col in enumerate((27, 28)):
                ts("dve", dg_t[:, 0:128], ident, prm[:, col:col + 1], ALU.mult, [R_prm, R_const], [dg_r])
                pt_, pr_ = PS.get()
                mm(pt_[:, 0:128], cst(C_ONES, 128), dg_t[:, 0:128], [R_const, dg_r], pr_)
                cp("dve", LBB[:, j * 128:(j + 1) * 128], pt_[:, 0:128], [pr_], [R_lbb])
            dma("sp", lambda h: h.dma_start(out=BGB[:, :], in_=bg_d[l].partition_broadcast(128)), [R_in], [R_bgb])
            dma("sp", lambda h: h.dma_start(out=WG2[:, :], in_=wg2_d[l]), [R_in], [R_wg2])

        TMPF = Pool(cx, es, "tmpf", [128, 512], F32, 4)

        def norm_half(xT, R_xT, W, tls, t0, gcol, dst_fn, rdst):
            for (g0, w) in tls:
                lo = g0 - t0
                sq_t, sq_r = SQ.get()
                op("dve", lambda h: h.tensor_tensor(sq_t[:, :, 0:w], xT[:, :, lo:lo + w], xT[:, :, lo:lo + w], ALU.mult), [R_xT], [sq_r])
                pt_, pr_ = PS.get()
                for c in range(8):
                    mm(pt_[:, 0:w], ONESB[:, :], sq_t[:, c, 0:w], [R_onesb, sq_r], pr_, start=(c == 0), stop=(c == 7))
                rs_t, rs_r = TMPF.get()
                act(rs_t[:, 0:w], pt_[:, 0:w], AF.Ln, [pr_, R_const], [rs_r], bias=CONST[:, C_EPS:C_EPS + 1], scale=1.0 / D)
                act(rs_t[:, 0:w], rs_t[:, 0:w], AF.Exp, [rs_r], [rs_r], scale=-0.5)
                for c in range(8):
                    stt("dve", dst_fn(c, lo, w), xT[:, c, lo:lo + w], prm[:, gcol + c:gcol + c + 1], rs_t[:, 0:w],
                        ALU.mult, ALU.mult, [R_xT, R_prm, rs_r], [rdst])

        wst_cur = [None]

        def load_w(dst_ap, src_ap, shape3, rdst):
            st_t, st_r = wst_cur[0].get()
            a, b = shape3
            view = st_t[:, 0:a * b].rearrange("p (a b) -> p a b", a=a)
            dma("sp", lambda h: h.dma_start(out=view, in_=src_ap), [R_in], [st_r])
            cp("pool", dst_ap, view, [st_r], [rdst])

        SQ = Pool(cx, es, "sq", [128, 8, 512], BF16, 1)

        def token_pre_layer0():
            with ExitStack() as ph:
                XT, R_XT = single(cx, ph, "XT0", [128, 8, 1040], F32)
                XN, R_XN = single(cx, ph, "XN0", [128, 8, 1040], BF16)
                XL = Pool(cx, ph, "xl", [128, D], F32, 2)
                for (t0, W, tls) in halves:
                    for (g0, w) in tls:
                        for u in range(0, w, 128):
                            n = min(128, w - u)
                            xl_t, xl_r = XL.get()
                            src = xp[g0 + u:g0 + u + n, :] if g0 < 2048 else xs[0:16, :]
                            dma("sp", lambda h: h.dma_start(out=xl_t[0:n, :], in_=src), [R_in], [xl_r])
                            for cc in range(0, 8, 4):
                                pt_, pr_ = PS.get()
                                for c in range(cc, cc + 4):
                                    tp(pt_[:, (c - cc) * 128:(c - cc) * 128 + n], xl_t[0:n, c * 128:(c + 1) * 128], n, [xl_r], pr_)
                                lo = g0 - t0 + u
                                cp("act", XT[:, cc:cc + 4, lo:lo + n],
                                   pt_[:, :].rearrange("p (c t) -> p c t", c=4)[:, :, 0:n], [pr_], [R_XT])
                    dma("pool", lambda h: h.dma_start(out=xres.rearrange("(c p) t -> p c t", p=128)[:, :, t0:t0 + W], in_=XT[:, :, 0:W]), [R_XT], [R_xres])
                    norm_half(XT, R_XT, W, tls, t0, 0, lambda c, lo, w: XN[:, c, lo:lo + w], R_XN)
                    dma("pool", lambda h: h.dma_start(out=a1.rearrange("(c p) t -> p c t", p=128)[:, :, t0:t0 + W], in_=XN[:, :, 0:W]), [R_XN], [R_a1])
                cx.barrier()

        def head_phase(l):
            with ExitStack() as ph:
                wst_cur[0] = Pool(cx, ph, "wsth", [128, 1024], F32, 2)
                WF, R_WF = single(cx, ph, "WF", [128, 8, NF], BF16)
                WT, R_WT = single(cx, ph, "WT", [128, 8, NT_], BF16)
                for c in range(8):
                    load_w(WF[:, c:c + 1, :], wF_d[l, c * 128:(c + 1) * 128, :].rearrange("p (a n) -> p a n", a=1), (1, NF), R_WF)
                    load_w(WT[:, c:c + 1, :], wT_d[l, c * 128:(c + 1) * 128, :].rearrange("p (a n) -> p a n", a=1), (1, NT_), R_WT)
                XNT = Pool(cx, ph, "xnt", [128, 8, 512], BF16, 2)
                ZF = {}
                for (nm, c0, wd) in FG:
                    ZF[nm] = single(cx, ph, "zf_" + nm, [128, 512], BF16 if nm in ("fq",) else F32)
                ZT, R_ZT = single(cx, ph, "ZT", [128, 4, NT_], F32)
                OT = Pool(cx, ph, "ot", [128, 3, 512], BF16, 2)
                ORAW, R_ORAW = single(cx, ph, "ORAW", [128, 512], F32)
                T128 = Pool(cx, ph, "t128", [128, 128], F32, 16)
                B128 = Pool(cx, ph, "b128", [128, 128], BF16, 16)
                TE = Pool(cx, ph, "te", [128, 260], F32, 4)
                GATE = Pool(cx, ph, "gate", [128, 512], F32, 3)
                ORAWH, R_ORAWH = single(cx, ph, "ORAWH", [128, 512], F32)
                PSG = SubPool(PS.tiles[0:3])
                PSH = SubPool(PS.tiles[3:6])
                SG, R_SG = single(cx, ph, "SG", [64, 128], F32)
                SGB, R_SGB = single(cx, ph, "SGB", [64, 128], BF16)
                SH, R_SH = single(cx, ph, "SH", [128, 128], F32)
                SHB, R_SHB = single(cx, ph, "SHB", [128, 128], BF16)

                def fproj(xn_t, xn_r, w):
                    for (nm, c0, wd) in FG:
                        pt_, pr_ = PS.get()
                        for c in range(8):
                            mm(pt_[0:wd, 0:w], WF[:, c, c0:c0 + wd], xn_t[:, c, 0:w], [R_WF, xn_r], pr_, start=(c == 0), stop=(c == 7))
                        z_t, z_r = ZF[nm]
                        cp("act", z_t[0:wd, 0:w], pt_[0:wd, 0:w], [pr_], [z_r])

                def tproj(xn_t, xn_r, col0, n, u):
                    for (a, b) in ((0, TA), (TA, NT_)):
                        pt_, pr_ = PS.get()
                        for c in range(8):
                            mm(pt_[0:n, 0:b - a], xn_t[:, c, col0:col0 + n], WT[:, c, a:b], [R_WT, xn_r], pr_, start=(c == 0), stop=(c == 7))
                        cp("dve", ZT[0:n, u, a:b], pt_[0:n, 0:b - a], [pr_], [R_ZT])

                def logsig(out_ap, in_ap, n, reads, wres, bias_ap):
                    act(out_ap, in_ap, AF.Exp, reads, [wres], bias=bias_ap, scale=-1.0)
                    act(out_ap, out_ap, AF.Ln, [wres], [wres], bias=1.0)
                    ts("dve", out_ap, out_ap, -1.0, ALU.mult, [wres], [wres])

                def scan_step(n, K, qT, kT, rs_qk, k_tok, v_tok, la, rs_tok, S, R_S, SB, R_SB, lnscale, out_ps_fn, PS=PS):
                    if n == 128:
                        ccat, ncat, csuf, cmask, nch, C = C_CAT2, 258, C_SUF2, C_MASK2, 2, 64
                    else:
                        ccat, ncat, csuf, cmask, nch, C = C_CAT4, 9, C_SUF4, C_MASK4, 1, 4
                    pb_t, pb_r = PS.get()
                    mm(pb_t[0:K, 0:ncat], la, cst(ccat, ncat, n), rs_tok + [R_const], pb_r)
                    psuf_t, psuf_r = PS.get()
                    mm(psuf_t[0:n, 0:K], cst(csuf, n, n), la, rs_tok + [R_const], psuf_r)
                    e_t, e_r = TE.get()
                    lb_ = 0.0 if lnscale is None else CONST[0:K, lnscale:lnscale + 1]
                    act(e_t[0:K, 0:n], pb_t[0:K, 0:n], AF.Exp, [pb_r, R_const], [e_r], bias=lb_)
                    act(e_t[0:K, 128:128 + n], pb_t[0:K, n:2 * n], AF.Exp, [pb_r, R_const], [e_r], bias=lb_)
                    ek_t, ek_r = T128.get()
                    act(ek_t[0:K, 0:n], pb_t[0:K, n:2 * n], AF.Exp, [pb_r], [ek_r], scale=-1.0)
                    act(e_t[0:K, 256:256 + nch], pb_t[0:K, 2 * n:2 * n + nch], AF.Exp, [pb_r], [e_r])
                    es_t, es_r = T128.get()
                    act(es_t[0:n, 0:K], psuf_t[0:n, 0:K], AF.Exp, [psuf_r], [es_r])
                    qb_t, qb_r = B128.get()
                    tt("dve", qb_t[0:K, 0:n], qT, e_t[0:K, 0:n], ALU.mult, rs_qk + [e_r], [qb_r])
                    qr_t, qr_r = B128.get()
                    tt("dve", qr_t[0:K, 0:n], qT, e_t[0:K, 128:128 + n], ALU.mult, rs_qk + [e_r], [qr_r])
                    kr_t, kr_r = B128.get()
                    tt("dve", kr_t[0:K, 0:n], kT, ek_t[0:K, 0:n], ALU.mult, rs_qk + [ek_r], [kr_r])
                    kh_t, kh_r = B128.get()
                    tt("dve", kh_t[0:n, 0:K], k_tok, es_t[0:n, 0:K], ALU.mult, rs_tok + [es_r], [kh_r])
                    vb_t, vb_r = B128.get()
                    cp("pool", vb_t[0:n, 0:128], v_tok, rs_tok, [vb_r])
                    yield
                    pa_t, pa_r = PS.get()
                    mm(pa_t[0:n, 0:n], kr_t[0:K, 0:n], qr_t[0:K, 0:n], [kr_r, qr_r], pa_r)
                    am_t, am_r = B128.get()
                    tt("dve", am_t[0:n, 0:n], pa_t[0:n, 0:n], cst(cmask, n, n), ALU.mult, [pa_r, R_const], [am_r])
                    yield
                    po_t, po_r = PS.get()
                    mm(po_t[:, 0:n], vb_t[0:n, 0:128], am_t[0:n, 0:n], [vb_r, am_r], po_r, start=True, stop=False)
                    for ci in range(nch):
                        mm(po_t[:, ci * C:(ci + 1) * C], SB[0:K, :], qb_t[0:K, ci * C:(ci + 1) * C], [R_SB, qb_r], po_r,
                           start=False, stop=(ci == nch - 1))
                        pS_t, pS_r = PS.get()
                        mm(pS_t[0:K, 0:128], kh_t[ci * C:ci * C + C, 0:K], vb_t[ci * C:ci * C + C, 0:128], [kh_r, vb_r], pS_r)
                        stt("dve", S[0:K, :], S[0:K, :], e_t[0:K, 256 + ci:257 + ci], pS_t[0:K, 0:128], ALU.mult, ALU.add,
                            [R_S, e_r, pS_r], [R_S])
                        cp("pool", SB[0:K, :], S[0:K, :], [R_S], [R_SB])
                        yield
                    out_ps_fn(po_t, po_r)

                def finish_o(w, nrm_col, gate_ap, gate_r, ot_t, ot_r, x, ORAW=ORAW, R_ORAW=R_ORAW, PS=PS):
                    sq_t, sq_r = B512.get()
                    tt("dve", sq_t[:, 0:w], ORAW[:, 0:w], ORAW[:, 0:w], ALU.mult, [R_ORAW], [sq_r])
                    pt_, pr_ = PS.get()
                    mm(pt_[:, 0:w], ONESB[:, :], sq_t[:, 0:w], [R_onesb, sq_r], pr_)
                    rs_t, rs_r = TMPF.get()
                    act(rs_t[:, 0:w], pt_[:, 0:w], AF.Ln, [pr_, R_const], [rs_r], bias=CONST[:, C_EPS:C_EPS + 1], scale=1.0 / 128)
                    act(rs_t[:, 0:w], rs_t[:, 0:w], AF.Exp, [rs_r], [rs_r], scale=-0.5)
                    stt("dve", rs_t[:, 0:w], ORAW[:, 0:w], prm[:, nrm_col:nrm_col + 1], rs_t[:, 0:w], ALU.mult, ALU.mult,
                        [R_ORAW, R_prm, rs_r], [rs_r])
                    tt("dve", ot_t[:, x, 0:w], rs_t[:, 0:w], gate_ap, ALU.mult, [rs_r, gate_r], [ot_r])

                def silu_gate(src_ap, src_r, w):
                    g_t, g_r = GATE.get()
                    sigmoid_inplace(g_t[:, 0:w], src_ap, [src_r], g_r)
                    tt("dve", g_t[:, 0:w], g_t[:, 0:w], src_ap, ALU.mult, [g_r, src_r], [g_r])
                    return g_t, g_r

                B512 = Pool(cx, ph, "b512", [128, 512], BF16, 4)
                scale = 128 ** -0.5

                def prep_hgrn_F(w):
                    hq_t, hq_r = ZF["hq"]
                    g_t, g_r = silu_gate(hq_t[:, 0:w], hq_r, w)
                    cp("dve", hq_t[:, 0:w], g_t[:, 0:w], [g_r], [hq_r])
                    hf_t, hf_r = ZF["hf"]
                    sigmoid_inplace(hf_t[:, 0:w], hf_t[:, 0:w], [hf_r], hf_r, neg=True)
                    ts("dve", hf_t[:, 0:w], hf_t[:, 0:w], prm[:, 28:29], ALU.mult, [hf_r, R_prm], [hf_r])

                def gla_tok(n, u, col0, PS=PS):
                    glr_t, glr_r = ZF["glr"]
                    pt_, pr_ = PS.get()
                    mm(pt_[0:n, 0:64], glr_t[0:16, col0:col0 + n], WG2[:, :], [glr_r, R_wg2], pr_)
                    la_t, la_r = T128.get()
                    tt("dve", la_t[0:n, 0:64], pt_[0:n, 0:64], BGB[0:n, :], ALU.add, [pr_, R_bgb], [la_r])
                    act(la_t[0:n, 0:64], la_t[0:n, 0:64], AF.Exp, [la_r], [la_r], scale=-1.0)
                    act(la_t[0:n, 0:64], la_t[0:n, 0:64], AF.Ln, [la_r], [la_r], bias=1.0)
                    ts("dve", la_t[0:n, 0:64], la_t[0:n, 0:64], -1.0 / 16.0, ALU.mult, [la_r], [la_r])
                    return la_t, la_r

                def hgrn_tok(n, u):
                    sg_t, sg_r = T128.get()
                    sigmoid_inplace(sg_t[0:n, :], ZT[0:n, u, 449:577], [R_ZT], sg_r)
                    t1_t, t1_r = T128.get()
                    tt("dve", t1_t[0:n, :], sg_t[0:n, :], LBB[0:n, 128:256], ALU.mult, [sg_r, R_lbb], [t1_r])
                    la_t, la_r = T128.get()
                    tt("dve", la_t[0:n, :], t1_t[0:n, :], LBB[0:n, 0:128], ALU.add, [t1_r, R_lbb], [la_r])
                    act(la_t[0:n, :], la_t[0:n, :], AF.Ln, [la_r], [la_r])
                    kt_t, kt_r = T128.get()
                    tt("dve", kt_t[0:n, :], LBB[0:n, 128:256], t1_t[0:n, :], ALU.subtract, [t1_r, R_lbb], [kt_r])
                    return la_t, la_r, kt_t, kt_r

                def sample_sweep(sm):
                    KGP = Pool(cx, sm, "kg", [128, 512], F32, 8)
                    VGP = Pool(cx, sm, "vg", [128, 512], F32, 8)
                    KTS, R_KTS = single(cx, sm, "KTS", [128, 16 * 128], BF16)
                    PTQ, R_PTQ = single(cx, sm, "PTQ", [128, 16, 16], I32)
                    PTQF, R_PTQF = single(cx, sm, "PTQF", [128, 16, 16], F32)
                    IDX, R_IDX = single(cx, sm, "IDX", [128, 16, 16], I32)
                    LFG, R_LFG = single(cx, sm, "LFG", [128, 16, 4], F32)
                    RB, R_RB = single(cx, sm, "RB", [128, 16, 4], F32)
                    R1, R_R1 = single(cx, sm, "R1", [128, 16, 4], F32)
                    RS, R_RS = single(cx, sm, "RS", [128, 16], F32)
                    SBS, R_SBS = single(cx, sm, "SBS", [128, 16, 4], F32)
                    PTS, R_PTS = single(cx, sm, "PTS", [128, 64, 4], F32)
                    ORAW2, R_ORAW2 = single(cx, sm, "ORAW2", [128, 16], F32)
                    SM4 = Pool(cx, sm, "sm4", [128, 132], F32, 8)
                    FKB, R_FKB = single(cx, sm, "FKB", [128, 16], BF16)
                    for g4 in range(4):
                        dma("sp", lambda h: h.dma_start(out=PTQ[32 * g4:32 * g4 + 32, :, :], in_=pt_d[:, g4::4].partition_broadcast(32)), [R_in], [R_PTQ])
                    cp("dve", PTQF[:, :, :], PTQ[:, :, :], [R_PTQ], [R_PTQF])
                    ts("dve", PTQF[:, :, :], PTQF[:, :, :], 32.0, ALU.mult, [R_PTQF, R_const], [R_PTQF], s2=CONST[:, C_PM32:C_PM32 + 1], op1=ALU.add)
                    ts("dve", PTQF[:, :, :], PTQF[:, :, :], float(l * NPOOL * 32), ALU.add, [R_PTQF], [R_PTQF])
                    cp("dve", IDX[:, :, :], PTQF[:, :, :], [R_PTQF], [R_IDX])

                    def issue_gather(gb, q):
                        kgs = [KGP.get() for _ in range(4)]
                        vgs = [VGP.get() for _ in range(4)]
                        for (src_d, lst) in ((ck_d, kgs), (cv_d, vgs)):
                            for Gi in range(4):
                                G = q * 4 + Gi
                                dst, rdst = lst[Gi]
                                dma("pool", lambda h: h.indirect_dma_start(out=dst[:, :], out_offset=None, in_=src_d,
                                                                           in_offset=bass.IndirectOffsetOnAxis(ap=IDX[:, gb, G:G + 1], axis=0)),
                                    [R_in, R_IDX], [rdst])
                        return kgs, vgs
                    for s in range(4):
                        xn_t, xn_r = XNT.get()
                        dma("sp", lambda h: h.dma_start(out=xn_t[:, :, 0:16], in_=b1[:, s * 128:(s + 1) * 128, 2048:2064].rearrange("c p t -> p c t")), [R_b1], [xn_r])
                        fproj(xn_t, xn_r, 16)
                        ot_t, ot_r = OT.get()
                        prep_hgrn_F(16)
                        gq_t, gq_r = ZF["gq"]
                        gk_t, gk_r = ZF["gk"]
                        hq_t, hq_r = ZF["hq"]
                        hf_t, hf_r = ZF["hf"]
                        fq_t, fq_r = ZF["fq"]
                        fk_t, fk_r = ZF["fk"]
                        cp("dve", FKB[:, 0:16], fk_t[:, 0:16], [fk_r], [R_FKB])
                        pending = issue_gather(4 * s, 0)
                        for bb in range(4):
                            gb = 4 * s + bb
                            c0 = 4 * bb
                            tproj(xn_t, xn_r, c0, 4, 0)
                            dma("pool", lambda h: h.dma_start(out=fk_s[l, gb], in_=ZT[0:4, 0, 192:320]), [R_ZT], [Res("o")])
                            dma("pool", lambda h: h.dma_start(out=fv_s[l, gb], in_=ZT[0:4, 0, 320:448]), [R_ZT], [Res("o")])
                            lfn_t, lfn_r = SM4.get()
                            logsig(lfn_t[0:4, 0:1], ZT[0:4, 0, 448:449], 4, [R_ZT, R_prm], lfn_r, prm[0:4, 26:27])
                            dma("pool", lambda h: h.dma_start(out=lf_s[l, gb].rearrange("(t o) -> t o", o=1), in_=lfn_t[0:4, 0:1]), [lfn_r], [Res("o")])
                            dma("sp", lambda h: h.dma_start(out=SG[:, :], in_=sgla_d[l, gb]), [R_in], [R_SG])
                            cp("pool", SGB[:, :], SG[:, :], [R_SG], [R_SGB])
                            la_t, la_r = gla_tok(4, 0, c0)

                            def outfa(po_t, po_r, c0=c0):
                                cp("act", ORAW[:, c0:c0 + 4], po_t[:, 0:4], [po_r], [R_ORAW])
                            for _ in scan_step(4, 64, gq_t[0:64, c0:c0 + 4], gk_t[0:64, c0:c0 + 4], [gq_r, gk_r],
                                               ZT[0:4, 0, 0:64], ZT[0:4, 0, 64:192], la_t[0:4, 0:64], [R_ZT, la_r],
                                               SG, R_SG, SGB, R_SGB, C_LN8, outfa):
                                pass
                            dma("pool", lambda h: h.dma_start(out=gla_s[l, gb], in_=SG[:, :]), [R_SG], [Res("o")])
                            dma("sp", lambda h: h.dma_start(out=SH[:, :], in_=shg_d[l, gb]), [R_in], [R_SH])
                            cp("pool", SHB[:, :], SH[:, :], [R_SH], [R_SHB])
                            la_t, la_r, kt_t, kt_r = hgrn_tok(4, 0)

                            def outfc(po_t, po_r, c0=c0):
                                cp("act", ORAW2[:, c0:c0 + 4], po_t[:, 0:4], [po_r], [R_ORAW2])
                            for _ in scan_step(4, 128, hq_t[:, c0:c0 + 4], hf_t[:, c0:c0 + 4], [hq_r, hf_r],
                                               kt_t[0:4, :], ZT[0:4, 0, 577:705], la_t[0:4, :], [R_ZT, la_r, kt_r],
                                               SH, R_SH, SHB, R_SHB, None, outfc):
                                pass
                            dma("pool", lambda h: h.dma_start(out=hg_s[l, gb], in_=SH[:, :]), [R_SH], [Res("o")])
                            lfg_rs = [Res("lfg") for _ in range(16)]
                            for G in range(16):
                                dma("pool", lambda h: h.indirect_dma_start(out=LFG[:, G, :], out_offset=None, in_=clf_d,
                                                                           in_offset=bass.IndirectOffsetOnAxis(ap=IDX[:, gb, G:G + 1], axis=0)),
                                    [R_in, R_IDX, R_LFG], [lfg_rs[G]])
                            R_LFGS = lfg_rs + [R_LFG]
                            op("dve", lambda h: h.tensor_reduce(RS[:, :], LFG[:, :, :], AX.X, ALU.add), R_LFGS, [R_RS])
                            op("dve", lambda h: h.memset(R1[:, :, 3:4], 0.0), [], [R_R1])
                            cp("dve", R1[:, :, 2:3], LFG[:, :, 3:4], R_LFGS, [R_R1])
                            tt("dve", R1[:, :, 1:2], LFG[:, :, 2:3], LFG[:, :, 3:4], ALU.add, R_LFGS, [R_R1])
                            tt("dve", R1[:, :, 0:1], R1[:, :, 1:2], LFG[:, :, 1:2], ALU.add, R_LFGS + [R_R1], [R_R1, R_LFG])
                            prt_t, prt_r = PS.get()
                            tp(prt_t[0:16, 0:128], RS[:, :], 128, [R_RS], prt_r)
                            ct_t, ct_r = SM4.get()
                            op("dve", lambda h: h.tensor_reduce(ct_t[0:16, 128:129], prt_t[0:16, 0:128], AX.X, ALU.add), [prt_r], [ct_r])
                            ts("dve", ct_t[0:16, 0:128], cst(C_ONES, 128, 16), ct_t[0:16, 128:129], ALU.mult, [ct_r, R_const], [ct_r])
                            pr2_t, pr2_r = PS.get()
                            mm(pr2_t[:, 0:16], cst(C_U128, 128), RS[:, :], [R_const, R_RS], pr2_r, start=True, stop=False)
                            mm(pr2_t[:, 0:16], ct_t[0:16, 0:128], cst(C_WG, 16, 16), [ct_r, R_const], pr2_r, start=False, stop=True)
                            cp("dve", RS[:, :], pr2_t[:, 0:16], [pr2_r], [R_RS])
                            tt("dve", RB[:, :, :], R1[:, :, :], RS[:, :].unsqueeze(2).to_broadcast([128, 16, 4]), ALU.add, [R_R1, R_RS], [R_RB])
                            po_t, po_r = PSA.get()
                            pd_t, pd_r = PSA.get()
                            for q in range(4):
                                kgs, vgs = pending
                                nxt = (gb, q + 1) if q < 3 else ((gb + 1, 0) if bb < 3 else None)
                                if nxt is not None:
                                    pending = issue_gather(*nxt)
                                for m4 in range(4):
                                    ptr_t, ptr_r = PS.get()
                                    for k in range(4):
                                        tp(ptr_t[:, k * 128:(k + 1) * 128], kgs[m4][0][:, k * 128:(k + 1) * 128], 128, [kgs[m4][1]], ptr_r)
                                    cp("act" if m4 % 2 == 0 else "dve", KTS[:, m4 * 512:(m4 + 1) * 512], ptr_t[:, :], [ptr_r], [R_KTS])
                                st_t, st_r = PS.get()
                                for m in range(16):
                                    mm(st_t[:, m * 4:(m + 1) * 4], KTS[:, m * 128:(m + 1) * 128], fq_t[:, c0:c0 + 4], [R_KTS, fq_r], st_r)
                                stt("dve", SBS[:, :, :], st_t[:, 0:64].rearrange("p (m q) -> p m q", q=4), scale,
                                    RB[:, q * 4:(q + 1) * 4, :].rearrange("p g t -> p (g t)").unsqueeze(2).to_broadcast([128, 16, 4]),
                                    ALU.mult, ALU.add, [st_r, R_RB], [R_SBS])
                                act(PTS[:, q * 16:(q + 1) * 16, :], SBS[:, :, :], AF.Exp, [R_SBS], [R_PTS])
                                for m in range(16):
                                    mm(po_t[0:4, 0:128], PTS[:, q * 16 + m, :], vgs[m // 4][0][:, (m % 4) * 128:(m % 4 + 1) * 128], [R_PTS, vgs[m // 4][1]], po_r,
                                       start=(q == 0 and m == 0), stop=False)
                            psn_t, psn_r = PS.get()
                            mm(psn_t[0:4, 0:4], FKB[:, c0:c0 + 4], fq_t[:, c0:c0 + 4], [R_FKB, fq_r], psn_r)
                            mm(psn_t[0:4, 8:9], cst(C_CAT4, 4, 4), lfn_t[0:4, 0:1], [R_const, lfn_r], psn_r)
                            nc_t, nc_r = SM4.get()
                            ts("dve", nc_t[0:4, 0:1], psn_t[0:4, 8:9], -1.0, ALU.mult, [psn_r], [nc_r])
                            act(nc_t[0:4, 4:8], psn_t[0:4, 0:4], AF.Exp, [psn_r, nc_r], [nc_r], bias=nc_t[0:4, 0:1], scale=scale)
                            tt("dve", nc_t[0:4, 4:8], nc_t[0:4, 4:8], cst(C_MASK4, 4, 4), ALU.mult, [nc_r, R_const], [nc_r])
                            mm(po_t[0:4, 0:128], nc_t[0:4, 4:8], ZT[0:4, 0, 320:448], [nc_r, R_ZT], po_r, start=False, stop=True)
                            ps_t, ps_r = SM4.get()
                            op("dve", lambda h: h.tensor_reduce(ps_t[:, 0:4], PTS[:, :, :].rearrange("p m q -> p q m"), AX.X, ALU.add), [R_PTS], [ps_r])
                            mm(pd_t[0:4, 0:1], ps_t[:, 0:4], cst(C_ONES, 1), [ps_r, R_const], pd_r, start=True, stop=False)
                            mm(pd_t[0:4, 0:1], nc_t[0:4, 4:8], cst(C_ONES, 1, 4), [nc_r, R_const], pd_r, start=False, stop=True)
                            ob_t, ob_r = SM4.get()
                            recip(ob_t[0:4, 128:129], pd_t[0:4, 0:1], [pd_r], [ob_r])
                            ts("dve", ob_t[0:4, 0:128], po_t[0:4, 0:128], ob_t[0:4, 128:129], ALU.mult, [po_r, ob_r], [ob_r])
                            pot_t, pot_r = PS.get()
                            tp(pot_t[:, 0:4], ob_t[0:4, 0:128], 4, [ob_r], pot_r)
                            cp("dve", ot_t[:, 1, c0:c0 + 4], pot_t[:, 0:4], [pot_r], [ot_r])
                        gr_t, gr_r = ZF["gr"]
                        g_t, g_r = silu_gate(gr_t[:, 0:16], gr_r, 16)
                        finish_o(16, 24, g_t[:, 0:16], g_r, ot_t, ot_r, 0)
                        hg_t, hg_r = ZF["hg"]
                        g_t, g_r = silu_gate(hg_t[:, 0:16], hg_r, 16)
                        finish_o(16, 25, g_t[:, 0:16], g_r, ot_t, ot_r, 2, ORAW=ORAW2, R_ORAW=R_ORAW2)
                        dma("pool", lambda h: h.dma_start(out=a2[s * 384:(s + 1) * 384, 2048:2064].rearrange("(x p) t -> p x t", p=128), in_=ot_t[:, :, 0:16]),
                            [ot_r], [Res("a2s")])

                pp = ExitStack()
                KT, R_KT = single(cx, pp, "KT", [128, SEQ], BF16)
                VP, R_VP = single(cx, pp, "VP", [128, 64, 128], BF16)
                LF, R_LF = single(cx, pp, "LF", [128, 64], F32)
                NEGC, R_NEGC = single(cx, pp, "NEGC", [128, 64], F32)
                NB, R_NB = single(cx, pp, "NB", [128, 64], F32)
                CAR, R_CAR = single(cx, pp, "CAR", [128, 2], F32)
                PT = Pool(cx, pp, "pt", [128, 512], BF16, 3)
                DEN, R_DEN = single(cx, pp, "DEN", [128, 512], F32)
                op("dve", lambda h: h.memset(SG[:, :], 0.0), [], [R_SG])
                op("dve", lambda h: h.memset(SGB[:, :], 0.0), [], [R_SGB])
                op("dve", lambda h: h.memset(SH[:, :], 0.0), [], [R_SH])
                op("dve", lambda h: h.memset(SHB[:, :], 0.0), [], [R_SHB])
                op("dve", lambda h: h.memset(CAR[:, :], 0.0), [], [R_CAR])
                for s in range(4):
                    for i in range(4):
                        ti = s * 4 + i
                        g0 = s * 2048 + i * 512
                        xn_t, xn_r = XNT.get()
                        dma("sp", lambda h: h.dma_start(out=xn_t[:, :, :], in_=b1[:, s * 128:(s + 1) * 128, i * 512:(i + 1) * 512].rearrange("c p t -> p c t")),
                            [R_b1], [xn_r])
                        fproj(xn_t, xn_r, 512)
                        for u in range(4):
                            tproj(xn_t, xn_r, u * 128, 128, u)
                        if DEBUG and l == 0 and ti == 0:
                            dma("sp", lambda h: h.dma_start(out=dbg_xn, in_=xn_t[:, :, :]), [xn_r], [Res("o")])
                            dma("sp", lambda h: h.dma_start(out=dbg_wt, in_=WT[:, :, :]), [R_WT], [Res("o")])
                            dma("sp", lambda h: h.dma_start(out=dbg_zt, in_=ZT[:, :, :]), [R_ZT], [Res("o")])
                        fk_t, fk_r = ZF["fk"]
                        cp("pool", KT[:, g0:g0 + 512], fk_t[:, 0:512], [fk_r], [R_KT])
                        cp("pool", VP[:, ti * 4:ti * 4 + 4, :], ZT[:, :, 320:448], [R_ZT], [R_VP])
                        dma("pool", lambda h: h.dma_start(out=fk_p[l, g0:g0 + 512, :].rearrange("(u p) d -> p u d", p=128), in_=ZT[:, :, 192:320]), [R_ZT], [Res("o")])
                        dma("pool", lambda h: h.dma_start(out=fv_p[l, g0:g0 + 512, :].rearrange("(u p) d -> p u d", p=128), in_=ZT[:, :, 320:448]), [R_ZT], [Res("o")])
                        logsig(LF[:, ti * 4:ti * 4 + 4], ZT[:, :, 448], 128, [R_ZT], R_LF, prm[:, 26:27])
                        pc_t, pc_r = PS.get()
                        mm(pc_t[:, 0:4], cst(C_INC128, 128), LF[:, ti * 4:ti * 4 + 4], [R_const, R_LF], pc_r)
                        mm(pc_t[:, 4:8], cst(C_ONES, 128), LF[:, ti * 4:ti * 4 + 4], [R_const, R_LF], pc_r)
                        cp("dve", CAR[:, 1:2], CAR[:, 0:1], [R_CAR], [R_CAR])
                        for k in range(4):
                            blk = ti * 4 + k
                            stt("dve", NEGC[:, blk:blk + 1], pc_t[:, k:k + 1], -1.0, CAR[:, 0:1], ALU.mult, ALU.subtract,
                                [pc_r, R_CAR], [R_NEGC])
                            tt("dve", CAR[:, 0:1], CAR[:, 0:1], pc_t[:, 4 + k:5 + k], ALU.add, [R_CAR, pc_r], [R_CAR])
                        nblk = ti * 4 + 4
                        ts("dve", NB[:, 0:nblk], NEGC[:, 0:nblk], CAR[:, 1:2], ALU.add, [R_NEGC, R_CAR], [R_NB])
                        ot_t, ot_r = OT.get()
                        gq_t, gq_r = ZF["gq"]
                        gk_t, gk_r = ZF["gk"]
                        hq_t, hq_r = ZF["hq"]
                        hf_t, hf_r = ZF["hf"]

                        def gen_gla():
                            for u in range(4):
                                la_t, la_r = gla_tok(128, u, u * 128, PSG)
                                yield

                                def outfn(po_t, po_r, u=u):
                                    cp("act", ORAW[:, u * 128:(u + 1) * 128], po_t[:, 0:128], [po_r], [R_ORAW])
                                yield from scan_step(128, 64, gq_t[0:64, u * 128:(u + 1) * 128], gk_t[0:64, u * 128:(u + 1) * 128], [gq_r, gk_r],
                                                     ZT[:, u, 0:64], ZT[:, u, 64:192], la_t[:, 0:64], [R_ZT, la_r],
                                                     SG, R_SG, SGB, R_SGB, C_LN8, outfn, PS=PSG)
                            gr_t, gr_r = ZF["gr"]
                            g_t, g_r = silu_gate(gr_t[:, 0:512], gr_r, 512)
                            yield
                            finish_o(512, 24, g_t[:, 0:512], g_r, ot_t, ot_r, 0, PS=PSG)

                        def gen_hgrn():
                            prep_hgrn_F(512)
                            yield
                            for u in range(4):
                                la_t, la_r, kt_t, kt_r = hgrn_tok(128, u)
                                yield

                                def outfn(po_t, po_r, u=u):
                                    cp("act", ORAWH[:, u * 128:(u + 1) * 128], po_t[:, 0:128], [po_r], [R_ORAWH])
                                yield from scan_step(128, 128, hq_t[:, u * 128:(u + 1) * 128], hf_t[:, u * 128:(u + 1) * 128], [hq_r, hf_r],
                                                     kt_t[:, :], ZT[:, u, 577:705], la_t[:, :], [R_ZT, la_r, kt_r],
                                                     SH, R_SH, SHB, R_SHB, None, outfn, PS=PSH)
                            hg_t, hg_r = ZF["hg"]
                            g_t, g_r = silu_gate(hg_t[:, 0:512], hg_r, 512)
                            yield
                            finish_o(512, 25, g_t[:, 0:512], g_r, ot_t, ot_r, 2, ORAW=ORAWH, R_ORAW=R_ORAWH, PS=PSH)

                        fq_t, fq_r = ZF["fq"]
                        pacc_t, pacc_r = PSA.tiles[0]
                        PSF = SubPool([PSA.tiles[1]])

                        def gen_fox():
                            for j in range(nblk):
                                pst_t, pst_r = PSF.get()
                                mm(pst_t[:, :], KT[:, j * 128:(j + 1) * 128], fq_t[:, 0:512], [R_KT, fq_r], pst_r)
                                yield
                                p_t, p_r = PT.get()
                                act(p_t[:, :], pst_t[:, :], AF.Exp, [pst_r, R_NB], [p_r], bias=NB[:, j:j + 1], scale=scale)
                                if j >= ti * 4:
                                    r = j - ti * 4
                                    tt("pool", p_t[:, :], p_t[:, :], MASKDB[:, r * 512:(r + 1) * 512], ALU.mult, [p_r, R_maskdb], [p_r])
                                mm(pacc_t[:, :], VP[:, j, :], p_t[:, :], [R_VP, p_r], pacc_r, start=(j == 0), stop=(j == nblk - 1))
                                if j == 0:
                                    cp("dve", DEN[:, :], p_t[:, :], [p_r], [R_DEN])
                                else:
                                    tt("dve", DEN[:, :], DEN[:, :], p_t[:, :], ALU.add, [R_DEN, p_r], [R_DEN])
                                yield
                            pden_t, pden_r = PSF.get()
                            mm(pden_t[:, :], cst(C_ONES, 128), DEN[:, :], [R_const, R_DEN], pden_r)
                            rd_t, rd_r = TMPF.get()
                            recip(rd_t[:, :], pden_t[:, :], [pden_r], [rd_r])
                            tt("dve", ot_t[:, 1, :], pacc_t[:, :], rd_t[:, :], ALU.mult, [pacc_r, rd_r], [ot_r])

                        gens = [gen_gla(), gen_hgrn(), gen_fox()]
                        if os.environ.get('MK_SEQ', '0') == '1':
                            for g_ in gens:
                                for _ in g_:
                                    pass
                            gens = []
                        while gens:
                            for g_ in list(gens):
                                try:
                                    next(g_)
                                except StopIteration:
                                    gens.remove(g_)
                        dma("pool", lambda h: h.dma_start(out=a2[s * 384:(s + 1) * 384, i * 512:(i + 1) * 512].rearrange("(x p) t -> p x t", p=128), in_=ot_t[:, :, :]),
                            [ot_r], [Res("a2s")])
                dma("pool", lambda h: h.dma_start(out=gla_p[l], in_=SG[:, :]), [R_SG], [Res("o")])
                dma("pool", lambda h: h.dma_start(out=hg_p[l], in_=SH[:, :]), [R_SH], [Res("o")])
                plf_t, plf_r = PS.get()
                tp(plf_t[0:64, 0:128], LF[:, :], 128, [R_LF], plf_r)
                lfo_t, lfo_r = T128.get()
                cp("dve", lfo_t[0:64, :], plf_t[0:64, 0:128], [plf_r], [lfo_r])
                dma("pool", lambda h: h.dma_start(out=lf_p[l], in_=lfo_t[0:64, :]), [lfo_r], [Res("o")])
                cx.barrier()
                pp.close()
                if os.environ.get('MK_NOSAMPLE', '0') != '1':
                    with ExitStack() as sm:
                        sample_sweep(sm)
                        cx.barrier()

        RIDX, R_ridx = single(cx, es, "RIDX", [128, 24], I32)
        dma("sp", lambda h: h.dma_start(out=RIDX[:, :], in_=ridx_d), [R_in], [R_ridx])

        PS_ALL = PS

        def token_post(l, last):
            with ExitStack() as ph:
                XT, R_XT = single(cx, ph, "XT", [128, 8, 1040], F32)
                MT, R_MT = single(cx, ph, "MT", [128, 8, 1040], BF16)
                HT, R_HT = single(cx, ph, "HT", [128, 8, 1040], BF16)
                UB, _ = single(cx, ph, "UB", [128, 22, 1040], BF16)
                R_OTG, R_XN, R_ACT = Res("otg"), Res("xn"), Res("act")
                WB = Pool(cx, ph, "wb", [128, 3072], BF16, 4)
                PS = SubPool(PS_ALL.tiles + PSA.tiles)
                wst_cur[0] = Pool(cx, ph, "wstt", [128, 2816], F32, 3)
                YL = Pool(cx, ph, "yl", [128, D], F32, 1)
                for (t0, W, tls) in halves:
                    dma("sp", lambda h: h.dma_start(out=XT[:, :, 0:W], in_=xres.rearrange("(c p) t -> p c t", p=128)[:, :, t0:t0 + W]), [R_xres], [R_XT])
                    dma("sp", lambda h: h.dma_start(out=UB[:, 12:20, 0:W], in_=a1.rearrange("(c p) t -> p c t", p=128)[:, :, t0:t0 + W]), [R_a1], [R_XN])
                    hv = 0 if t0 == 0 else 1
                    otg_rs = [Res("otg") for _ in range(12)]
                    for s_ in range(4):
                        for x in range(3):
                            col = hv * 12 + s_ * 3 + x
                            dma("pool", lambda h: h.indirect_dma_start(out=UB[:, x * 4 + s_, 0:W], out_offset=None, in_=b2.rearrange("q r (two t) -> (q r two) t", two=2),
                                                                       in_offset=bass.IndirectOffsetOnAxis(ap=RIDX[:, col:col + 1], axis=0)),
                                [R_b2, R_ridx, R_OTG], [otg_rs[x * 4 + s_]])
                    if DEBUG and l == 0:
                        dma("sp", lambda h: h.dma_start(out=dbg_o[:, :, t0:t0 + W], in_=UB[:, 0:12, 0:W]), otg_rs, [Res("o")])
                    for c in range(8):
                        wg_t, wg_r = WB.get()
                        wgv = wg_t[:, :].rearrange("p (k x n) -> p k x n", k=8, x=3)
                        for x in range(3):
                            load_w(wgv[:, :, x, :], wG_d[l].rearrange("(k p) n -> p k n", p=128)[:, :, x * D + c * 128:x * D + (c + 1) * 128], (8, 128), wg_r)
                        wb_t, wb_r = WB.get()
                        wbv = wb_t[:, 0:1536].rearrange("p (x h n) -> p x h n", x=3, h=4)
                        for x, wd in enumerate((wba_d, wbb_d, wbc_d)):
                            load_w(wbv[:, x, :, :], wd[l].rearrange("(h p) n -> p h n", p=128)[:, :, c * 128:(c + 1) * 128], (4, 128), wb_r)
                        for (g0, w) in tls:
                            lo = g0 - t0
                            m_t, m_r = TMPF.get()
                            for x in range(3):
                                pg_t, pg_r = PS.get()
                                for k in range(8):
                                    mm(pg_t[:, 0:w], wgv[:, k, x, :], UB[:, 12 + k, lo:lo + w], [wg_r, R_XN], pg_r, start=(k == 0), stop=(k == 7))
                                pb_t, pb_r = PS.get()
                                for hh in range(4):
                                    mm(pb_t[:, 0:w], wbv[:, x, hh, :], UB[:, x * 4 + hh, lo:lo + w], [wb_r, otg_rs[x * 4 + hh]], pb_r, start=(hh == 0), stop=(hh == 3))
                                e_t, e_r = TMPF.get()
                                sigmoid_inplace(e_t[:, 0:w], pg_t[:, 0:w], [pg_r], e_r)
                                if x == 0:
                                    tt("dve", m_t[:, 0:w], e_t[:, 0:w], pb_t[:, 0:w], ALU.mult, [e_r, pb_r], [m_r])
                                else:
                                    tt("dve", e_t[:, 0:w], e_t[:, 0:w], pb_t[:, 0:w], ALU.mult, [e_r, pb_r], [e_r])
                                    tt("dve", m_t[:, 0:w], m_t[:, 0:w], e_t[:, 0:w], ALU.add, [m_r, e_r], [m_r])
                            cp("act", MT[:, c, lo:lo + w], m_t[:, 0:w], [m_r], [R_MT])
                    if DEBUG and l == 0:
                        dma("sp", lambda h: h.dma_start(out=dbg_m[:, :, t0:t0 + W], in_=MT[:, :, 0:W]), [R_MT], [Res("o")])
                    for c in range(8):
                        wo_t, wo_r = WB.get()
                        wov = wo_t[:, 0:1024].rearrange("p (k n) -> p k n", k=8)
                        load_w(wov, wo_d[l].rearrange("(k p) n -> p k n", p=128)[:, :, c * 128:(c + 1) * 128], (8, 128), wo_r)
                        for (g0, w) in tls:
                            lo = g0 - t0
                            py_t, py_r = PS.get()
                            for k in range(8):
                                mm(py_t[:, 0:w], wov[:, k, :], MT[:, k, lo:lo + w], [wo_r, R_MT], py_r, start=(k == 0), stop=(k == 7))
                            tt("dve", XT[:, c, lo:lo + w], XT[:, c, lo:lo + w], py_t[:, 0:w], ALU.add, [R_XT, py_r], [R_XT])
                    norm_half(XT, R_XT, W, tls, t0, 8, lambda c, lo, w: HT[:, c, lo:lo + w], R_HT)
                    cx.barrier()
                    for j in range(22):
                        wf_t, wf_r = WB.get()
                        wfv = wf_t[:, 0:2048].rearrange("p (g k n) -> p g k n", g=2, k=8)
                        load_w(wfv[:, 0, :, :], wfg_d[l].rearrange("(k p) n -> p k n", p=128)[:, :, j * 128:(j + 1) * 128], (8, 128), wf_r)
                        load_w(wfv[:, 1, :, :], wfu_d[l].rearrange("(k p) n -> p k n", p=128)[:, :, j * 128:(j + 1) * 128], (8, 128), wf_r)
                        for (g0, w) in tls:
                            lo = g0 - t0
                            pg_t, pg_r = PS.get()
                            pu_t, pu_r = PS.get()
                            for k in range(8):
                                mm(pg_t[:, 0:w], wfv[:, 0, k, :], HT[:, k, lo:lo + w], [wf_r, R_HT], pg_r, start=(k == 0), stop=(k == 7))
                            for k in range(8):
                                mm(pu_t[:, 0:w], wfv[:, 1, k, :], HT[:, k, lo:lo + w], [wf_r, R_HT], pu_r, start=(k == 0), stop=(k == 7))
                            e_t, e_r = TMPF.get()
                            sigmoid_inplace(e_t[:, 0:w], pg_t[:, 0:w], [pg_r], e_r)
                            tt("dve", e_t[:, 0:w], e_t[:, 0:w], pg_t[:, 0:w], ALU.mult, [e_r, pg_r], [e_r])
                            tt("dve", UB[:, j, lo:lo + w], e_t[:, 0:w], pu_t[:, 0:w], ALU.mult, [e_r, pu_r], [R_ACT])
                    for c in range(8):
                        wd_t, wd_r = WB.get()
                        wdv = wd_t[:, 0:2816].rearrange("p (j n) -> p j n", j=22)
                        load_w(wdv, wfd_d[l].rearrange("(j p) n -> p j n", p=128)[:, :, c * 128:(c + 1) * 128], (22, 128), wd_r)
                        for (g0, w) in tls:
                            lo = g0 - t0
                            py_t, py_r = PS.get()
                            for j in range(22):
                                mm(py_t[:, 0:w], wdv[:, j, :], UB[:, j, lo:lo + w], [wd_r, R_ACT], py_r, start=(j == 0), stop=(j == 21))
                            tt("dve", XT[:, c, lo:lo + w], XT[:, c, lo:lo + w], py_t[:, 0:w], ALU.add, [R_XT, py_r], [R_XT])
                    cx.barrier()
                    if (not last) or DEBUG:
                        dma("pool", lambda h: h.dma_start(out=xres.rearrange("(c p) t -> p c t", p=128)[:, :, t0:t0 + W], in_=XT[:, :, 0:W]), [R_XT], [R_xres])
                        norm_half(XT, R_XT, W, tls, t0, 16, lambda c, lo, w: UB[:, 12 + c, lo:lo + w], R_XN)
                        dma("pool", lambda h: h.dma_start(out=a1.rearrange("(c p) t -> p c t", p=128)[:, :, t0:t0 + W], in_=UB[:, 12:20, 0:W]), [R_XN], [R_a1])
                    else:
                        norm_half(XT, R_XT, W, tls, t0, 16, lambda c, lo, w: XT[:, c, lo:lo + w], R_XT)
                        for (g0, w) in tls:
                            for u in range(0, w, 128):
                                n = min(128, w - u)
                                lo = g0 - t0 + u
                                yl_t, yl_r = YL.get()
                                for cc in range(0, 8, 4):
                                    pt_, pr_ = PS.get()
                                    for c in range(cc, cc + 4):
                                        tp(pt_[0:n, (c - cc) * 128:(c - cc + 1) * 128], XT[:, c, lo:lo + n], 128, [R_XT], pr_)
                                    cp("act", yl_t[0:n, cc * 128:(cc + 4) * 128], pt_[0:n, :], [pr_], [yl_r])
                                dst = y_p[g0 + u:g0 + u + n, :] if g0 < 2048 else y_s[0:16, :]
                                dma("pool", lambda h: h.dma_start(out=dst, in_=yl_t[0:n, :]), [yl_r], [Res("o")])
                    cx.barrier()

        def load_w4(dst_ap, src_ap, shape, rdst):
            st_t, st_r = WST.get()
            a, b, c_ = shape
            view = st_t[:, 0:a * b * c_].rearrange("p (a b c) -> p a b c", a=a, b=b)
            dma("sp", lambda h: h.dma_start(out=view, in_=src_ap), [R_in], [st_r])
            cp("pool", dst_ap, view, [st_r], [rdst])

        load_layer_params(0)
        token_pre_layer0()
        for l in range(2):
            if STAGE < 2:
                break
            if l > 0:
                load_layer_params(l)
            if os.environ.get('MK_NOAG', '0') != '1':
                allgather(a1, b1, R_a1, R_b1, 8)
            if DEBUG and l == 0:
                with ExitStack() as dph:
                    DB, R_DB = single(cx, dph, "DB", [128, 4, NTOK], BF16)
                    dma("sp", lambda h: h.dma_start(out=DB[:, :, :], in_=b1[0].rearrange("(r p) t -> p r t", p=128)), [R_b1], [R_DB])
                    dma("sp", lambda h: h.dma_start(out=dbg_b1, in_=DB[:, :, :]), [R_DB], [Res("o")])
                    dma("sp", lambda h: h.dma_start(out=DB[:, 0, :], in_=a1[0:128, :]), [R_a1, R_DB], [R_DB])
                    dma("sp", lambda h: h.dma_start(out=dbg_a1, in_=DB[:, 0, :]), [R_DB], [Res("o")])
                    cx.barrier()
            if STAGE < 3:
                break
            head_phase(l)
            if STAGE < 4:
                break
            allgather(a2, b2, R_a2, R_b2, 12)
            token_post(l, last=(l == 1))
            if STAGE < 5:
                break
        cx.final_wait()
    return nc


_NC = None
_LAST = None


def _prep(inputs):
    f32 = np.float32
    g = lambda k: np.asarray(inputs[k])
    w_in = g("w_in")
    sp = [256, 512, 1024, 1040, 1552, 2064, 2576, 3088, 3092, 3604, 4116, 4628, 5140, 6164, 7188]
    consts = make_consts()
    shared = {
        "consts": consts,
        "wG": np.ascontiguousarray(w_in[:, :, 5140:8212]),
        "wba": g("w_branch_a"), "wbb": g("w_branch_b"), "wbc": g("w_branch_c"), "wo": g("w_out"),
        "n1": g("norm1_g"), "n2": g("norm2_g"), "nf": g("final_norm_g"),
        "wfg": g("w_ffn_gate"), "wfu": g("w_ffn_up"), "wfd": g("w_ffn_down"),
        "gng": g("gla_norm_g"), "hng": g("hg_norm_g"),
    }
    ck, cv, clf = g("cache_fox_k"), g("cache_fox_v"), g("cache_fox_logf")
    per_head = []
    for h in range(4):
        gq = w_in[:, :, 0 + 64 * h:64 * h + 64]
        gk = w_in[:, :, 256 + 64 * h:256 + 64 * h + 64]
        gv = w_in[:, :, 512 + 128 * h:512 + 128 * h + 128]
        glr = w_in[:, :, 1024:1040]
        gr = w_in[:, :, 1040 + 128 * h:1040 + 128 * h + 128]
        fq = w_in[:, :, 1552 + 128 * h:1552 + 128 * h + 128]
        fk = w_in[:, :, 2064 + 128 * h:2064 + 128 * h + 128]
        fv = w_in[:, :, 2576 + 128 * h:2576 + 128 * h + 128]
        ff = w_in[:, :, 3088 + h:3089 + h]
        hq = w_in[:, :, 3092 + 128 * h:3092 + 128 * h + 128]
        hf = w_in[:, :, 3604 + 128 * h:3604 + 128 * h + 128]
        hi = w_in[:, :, 4116 + 128 * h:4116 + 128 * h + 128]
        hgg = w_in[:, :, 4628 + 128 * h:4628 + 128 * h + 128]
        wF = np.ascontiguousarray(np.concatenate([gq, gk, gr, glr, fq, fk, hq, hf, hgg], axis=2))
        wT = np.ascontiguousarray(np.concatenate([gk, gv, fk, fv, ff, hf, hi], axis=2))
        d = {
            "wF": wF, "wT": wT,
            "wg2": np.ascontiguousarray(g("gla_wg2")[:, :, 64 * h:64 * h + 64]),
            "bg": np.ascontiguousarray(g("gla_bg")[:, 64 * h:64 * h + 64]),
            "fbf": np.ascontiguousarray(g("fox_bf")[:, h:h + 1]),
            "hlb": np.ascontiguousarray(g("hg_lb_logits")[:, 128 * h:128 * h + 128]),
            "ck": np.ascontiguousarray(ck[:, :, :, h, :]).reshape(2 * NPOOL * 32, 512),
            "cv": np.ascontiguousarray(cv[:, :, :, h, :]).reshape(2 * NPOOL * 32, 512),
            "clf": np.ascontiguousarray(clf[:, :, :, h]).reshape(2 * NPOOL * 32, 4),
        }
        per_head.append(d)
    in_maps = []
    p = np.arange(128)
    for c in range(8):
        b, h = c // 4, c % 4
        m = dict(shared)
        m.update(per_head[h])
        m["xp"] = np.ascontiguousarray(g("x_prompt")[b, 2048 * h:2048 * h + 2048])
        m["xs"] = np.ascontiguousarray(g("x_sample")[16 * b + 4 * h:16 * b + 4 * h + 4].reshape(16, D))
        m["sgla"] = np.ascontiguousarray(g("state_gla")[:, 16 * b:16 * b + 16, h])
        m["shg"] = np.ascontiguousarray(g("state_hgrn")[:, 16 * b:16 * b + 16, h])
        m["pt"] = np.ascontiguousarray(g("page_table")[16 * b:16 * b + 16]).astype(np.int32)
        ridx = np.zeros((128, 24), np.int32)
        for hv in range(2):
            for s_ in range(4):
                for x in range(3):
                    ridx[:, hv * 12 + s_ * 3 + x] = 2 * ((h * 3 + x) * 512 + s_ * 128 + p) + hv
        m["ridx"] = ridx
        in_maps.append(m)
    return in_maps


def kernel(**inputs):
    global _NC
    if _NC is None:
        _NC = build_program()
    in_maps = _prep(inputs)
    res = run_bass_kernel_spmd(_NC, in_maps, core_ids=list(range(8)))
    R = res.results
    global _LAST
    _LAST = R
    f32 = np.float32
    y_p = np.zeros((2, SEQ, D), f32)
    y_s = np.zeros((32, 4, D), f32)
    gla_p = np.zeros((2, 2, 4, 64, 128), f32)
    gla_s = np.zeros((2, 32, 4, 64, 128), f32)
    fk_p = np.zeros((2, 2, SEQ, 4, 128), f32)
    fv_p = np.zeros((2, 2, SEQ, 4, 128), f32)
    lf_p = np.zeros((2, 2, SEQ, 4), f32)
    fk_s = np.zeros((2, 32, 4, 4, 128), f32)
    fv_s = np.zeros((2, 32, 4, 4, 128), f32)
    lf_s = np.zeros((2, 32, 4, 4), f32)
    hg_p = np.zeros((2, 2, 4, 128, 128), f32)
    hg_s = np.zeros((2, 32, 4, 128, 128), f32)
    for c in range(8):
        b, h = c // 4, c % 4
        r = R[c]
        y_p[b, 2048 * h:2048 * h + 2048] = r["y_p"]
        y_s[16 * b + 4 * h:16 * b + 4 * h + 4] = r["y_s"].reshape(4, 4, D)
        gla_p[:, b, h] = r["gla_p"]
        gla_s[:, 16 * b:16 * b + 16, h] = r["gla_s"]
        fk_p[:, b, :, h, :] = r["fk_p"]
        fv_p[:, b, :, h, :] = r["fv_p"]
        lf_p[:, b, :, h] = r["lf_p"].reshape(2, SEQ)
        fk_s[:, 16 * b:16 * b + 16, :, h, :] = r["fk_s"]
        fv_s[:, 16 * b:16 * b + 16, :, h, :] = r["fv_s"]
        lf_s[:, 16 * b:16 * b + 16, :, h] = r["lf_s"]
        hg_p[:, b, h] = r["hg_p"]
        hg_s[:, 16 * b:16 * b + 16, h] = r["hg_s"]
    return (y_p, y_s, gla_p, gla_s, fk_p, fv_p, lf_p, fk_s, fv_s, lf_s, hg_p, hg_s)
```

```python
import numpy as np
from contextlib import ExitStack
import concourse.bass as bass
import concourse.mybir as mybir
from concourse.bass_utils import run_bass_kernel_spmd

F32 = mybir.dt.float32
BF16 = mybir.dt.bfloat16
I32 = mybir.dt.int32
AF = mybir.ActivationFunctionType
ALU = mybir.AluOpType
AX = mybir.AxisListType

D = 1024
SEQ = 8192
NTOK = 2064
NPOOL = 2560
EPS = 1e-6
FH = 2816
SAME_ENGINE_SYNC = True
import os
STAGE = int(os.environ.get('MK_STAGE', '99'))
SMALL_CACHE = os.environ.get('MK_SMALLCACHE', '0') == '1'
DEBUG = os.environ.get('MK_DEBUG', '0') == '1'

FG = [("gq", 0, 64), ("gk", 64, 64), ("gr", 128, 128), ("glr", 256, 16), ("fq", 272, 128),
      ("fk", 400, 128), ("hq", 528, 128), ("hf", 656, 128), ("hg", 784, 128)]
NF = 912
NT_ = 705
TA = 448

C_ID = 0
C_CAT2 = 128
C_SUF2 = 386
C_MASK2 = 514
C_CAT4 = 642
C_SUF4 = 651
C_MASK4 = 655
C_U128 = 659
C_INC128 = 787
C_WG = 915
C_PM32 = 931
C_ONES = 932
C_MASKD = 1060
C_EPS = 1060 + 2048
C_LN8 = C_EPS + 1
NCONST = C_EPS + 4


def make_consts():
    c = np.zeros((128, NCONST), np.float32)
    p = np.arange(128)
    c[:, C_ID:C_ID + 128] = np.eye(128)
    s, t = p[:, None], p[None, :]
    same = (s // 64) == (t // 64)
    inc2 = (same & (s <= t)).astype(np.float32)
    mid = (t // 64) * 64 + 31
    rel2 = inc2 - (same & (s <= mid)).astype(np.float32)
    last2 = np.zeros((128, 2), np.float32)
    last2[:64, 0] = 1
    last2[64:, 1] = 1
    c[:, C_CAT2:C_CAT2 + 128] = inc2
    c[:, C_CAT2 + 128:C_CAT2 + 256] = rel2
    c[:, C_CAT2 + 256:C_CAT2 + 258] = last2
    c[:, C_SUF2:C_SUF2 + 128] = (same & (s > t)).astype(np.float32)
    c[:, C_MASK2:C_MASK2 + 128] = inc2
    s4, t4 = np.arange(4)[:, None], np.arange(4)[None, :]
    inc4 = (s4 <= t4).astype(np.float32)
    rel4 = inc4 - (s4 <= 1).astype(np.float32)
    c[:4, C_CAT4:C_CAT4 + 4] = inc4
    c[:4, C_CAT4 + 4:C_CAT4 + 8] = rel4
    c[:4, C_CAT4 + 8] = 1
    c[:4, C_SUF4:C_SUF4 + 4] = (s4 > t4).astype(np.float32)
    c[:4, C_MASK4:C_MASK4 + 4] = inc4
    c[:, C_U128:C_U128 + 128] = (s > t).astype(np.float32)
    c[:, C_INC128:C_INC128 + 128] = (s <= t).astype(np.float32)
    g = np.arange(16)
    c[:16, C_WG:C_WG + 16] = (g[:, None] > g[None, :]).astype(np.float32)
    c[:, C_PM32] = p % 32
    c[:, C_ONES:C_ONES + 128] = 1.0
    c[:, C_EPS] = EPS
    c[:, C_LN8] = np.log(0.125)
    f = np.arange(512)[None, :]
    for r in range(4):
        c[:, C_MASKD + r * 512:C_MASKD + (r + 1) * 512] = ((128 * r + p[:, None]) <= f).astype(np.float32)
    return c


class Res:
    __slots__ = ("name", "w", "rs")

    def __init__(self, name):
        self.name = name
        self.w = None
        self.rs = {}


class Eng:
    def __init__(self, name, handle, sem, in_order):
        self.name, self.h, self.sem, self.cnt = name, handle, sem, 0
        self.seen = {}
        self.in_order = in_order
        self.dsems = []
        self.dvals = []
        self.dnext = 0


class Ctx:
    def __init__(self, nc, es):
        self.nc = nc
        self.es = es
        self.engs = {}
        self.uid = 0
        self.all_dma = []

    def add_engine(self, name, handle, in_order=False, ndma=0):
        sem = self.es.enter_context(self.nc.semaphore("s_" + name))
        e = Eng(name, handle, sem, in_order)
        for i in range(ndma):
            e.dsems.append(self.es.enter_context(self.nc.semaphore("d_%s_%d" % (name, i))))
            e.dvals.append(0)
        self.engs[name] = e
        return e

    def _wait(self, e, tok):
        sem, val = tok
        k = id(sem)
        if e.seen.get(k, 0) >= val:
            return
        e.h.wait_ge(sem, val)
        e.seen[k] = val

    def _deps(self, e, reads, writes):
        deps = {}

        def add(tok):
            k = id(tok[0])
            if k not in deps or deps[k][1] < tok[1]:
                deps[k] = tok
        for r in reads:
            if r.w is not None:
                add(r.w)
        for w in writes:
            if w.w is not None:
                add(w.w)
            for tok in w.rs.values():
                add(tok)
        for k, tok in deps.items():
            if tok[0] is e.sem and (e.in_order or not SAME_ENGINE_SYNC):
                continue
            self._wait(e, tok)

    def _record(self, tok, reads, writes):
        k = id(tok[0])
        for r in reads:
            old = r.rs.get(k)
            if old is None or old[1] < tok[1]:
                r.rs[k] = tok
        for w in writes:
            w.w = tok
            w.rs = {}

    def op(self, en, fn, reads=(), writes=()):
        e = self.engs[en]
        self._deps(e, reads, writes)
        inst = fn(e.h)
        e.cnt += 1
        inst.then_inc(e.sem, 1)
        tok = (e.sem, e.cnt)
        e.seen[id(e.sem)] = e.cnt if e.in_order else e.seen.get(id(e.sem), 0)
        self._record(tok, reads, writes)
        return tok

    def dma(self, en, fn, reads=(), writes=()):
        e = self.engs[en]
        self._deps(e, reads, writes)
        i = e.dnext
        e.dnext = (e.dnext + 1) % len(e.dsems)
        sem = e.dsems[i]
        if e.dvals[i] > 0:
            self._wait(e, (sem, e.dvals[i]))
        inst = fn(e.h)
        e.dvals[i] += 16
        inst.then_inc(sem, 16)
        tok = (sem, e.dvals[i])
        self._record(tok, reads, writes)
        return tok

    def barrier(self):
        toks = []
        for e in self.engs.values():
            if e.cnt > 0:
                toks.append((e.sem, e.cnt))
            for s, v in zip(e.dsems, e.dvals):
                if v > 0:
                    toks.append((s, v))
        for e in self.engs.values():
            for tok in toks:
                if tok[0] is e.sem:
                    continue
                self._wait(e, tok)

    def final_wait(self):
        e = self.engs["sp"]
        for o in self.engs.values():
            if o.cnt > 0 and o is not e:
                self._wait(e, (o.sem, o.cnt))
            for s, v in zip(o.dsems, o.dvals):
                if v > 0:
                    self._wait(e, (s, v))


class Pool:
    def __init__(self, cx, es, name, shape, dtype, bufs, psum=False):
        self.tiles = []
        for i in range(bufs):
            cx.uid += 1
            nm = "%s_%d_%d" % (name, i, cx.uid)
            if psum:
                t = es.enter_context(cx.nc.psum_tensor(nm, shape, dtype))
            else:
                t = es.enter_context(cx.nc.sbuf_tensor(nm, shape, dtype))
            self.tiles.append((t, Res(nm)))
        self.i = 0

    def get(self):
        t = self.tiles[self.i]
        self.i = (self.i + 1) % len(self.tiles)
        return t


class SubPool:
    def __init__(self, tiles):
        self.tiles = list(tiles)
        self.i = 0

    def get(self):
        t = self.tiles[self.i]
        self.i = (self.i + 1) % len(self.tiles)
        return t


def single(cx, es, name, shape, dtype):
    cx.uid += 1
    name = "%s_%d" % (name, cx.uid)
    t = es.enter_context(cx.nc.sbuf_tensor(name, shape, dtype))
    return t, Res(name)


def build_program():
    nc = bass.Bass("TRN2", target_bir_lowering=False)

    def din(name, shape, dt=F32):
        return nc.dram_tensor(name, shape, dt, kind="ExternalInput").ap()

    def dout(name, shape, dt=F32):
        return nc.dram_tensor(name, shape, dt, kind="ExternalOutput").ap()

    def dint(name, shape, dt):
        return nc.dram_tensor(name, shape, dt, kind="Internal").ap()

    xp = din("xp", [2048, D])
    xs = din("xs", [16, D])
    consts_d = din("consts", [128, NCONST])
    wF_d = din("wF", [2, D, NF])
    wT_d = din("wT", [2, D, NT_])
    wG_d = din("wG", [2, D, 3 * D])
    wg2_d = din("wg2", [2, 16, 64])
    bg_d = din("bg", [2, 64])
    gng_d = din("gng", [2, 128])
    hng_d = din("hng", [2, 128])
    fbf_d = din("fbf", [2, 1])
    hlb_d = din("hlb", [2, 128])
    wba_d = din("wba", [2, 512, D])
    wbb_d = din("wbb", [2, 512, D])
    wbc_d = din("wbc", [2, 512, D])
    wo_d = din("wo", [2, D, D])
    n1_d = din("n1", [2, D])
    n2_d = din("n2", [2, D])
    nf_d = din("nf", [D])
    wfg_d = din("wfg", [2, D, FH])
    wfu_d = din("wfu", [2, D, FH])
    wfd_d = din("wfd", [2, FH, D])
    sgla_d = din("sgla", [2, 16, 64, 128])
    shg_d = din("shg", [2, 16, 128, 128])
    NROWS = 64 if SMALL_CACHE else 2 * NPOOL * 32
    ck_d = din("ck", [NROWS, 512])
    cv_d = din("cv", [NROWS, 512])
    clf_d = din("clf", [NROWS, 4])
    pt_d = din("pt", [16, 64], I32)
    ridx_d = din("ridx", [128, 24], I32)

    y_p = dout("y_p", [2048, D])
    y_s = dout("y_s", [16, D])
    gla_p = dout("gla_p", [2, 64, 128])
    gla_s = dout("gla_s", [2, 16, 64, 128])
    fk_p = dout("fk_p", [2, SEQ, 128])
    fv_p = dout("fv_p", [2, SEQ, 128])
    lf_p = dout("lf_p", [2, 64, 128])
    fk_s = dout("fk_s", [2, 16, 4, 128])
    fv_s = dout("fv_s", [2, 16, 4, 128])
    lf_s = dout("lf_s", [2, 16, 4])
    hg_p = dout("hg_p", [2, 128, 128])
    hg_s = dout("hg_s", [2, 16, 128, 128])

    a1 = dint("a1", [D, NTOK], BF16)
    b1 = dint("b1", [8, 512, NTOK], BF16)
    a2 = dint("a2", [4 * 384, NTOK], BF16)
    b2 = dint("b2", [12, 512, NTOK], BF16)
    xres = dout("xres", [D, NTOK]) if DEBUG else dint("xres", [D, NTOK], F32)
    dbg_o = dout("dbg_o", [128, 12, NTOK], BF16) if DEBUG else None
    dbg_m = dout("dbg_m", [128, 8, NTOK], BF16) if DEBUG else None
    dbg_raw = dout("dbg_raw", [128, 512]) if DEBUG else None
    dbg_b1 = dout("dbg_b1", [128, 4, NTOK], BF16) if DEBUG else None
    dbg_a1 = dout("dbg_a1", [128, NTOK], BF16) if DEBUG else None
    dbg_xn = dout("dbg_xn", [128, 8, 512], BF16) if DEBUG else None
    dbg_wt = dout("dbg_wt", [128, 8, NT_], BF16) if DEBUG else None
    dbg_zt = dout("dbg_zt", [128, 4, NT_]) if DEBUG else None
    dbg_gate = dout("dbg_gate", [128, 512]) if DEBUG else None
    R_a1, R_b1, R_a2, R_b2, R_xres = Res("a1"), Res("b1"), Res("a2"), Res("b2"), Res("xres")
    R_out = Res("outs")
    R_in = Res("ins")
    groups = [[0, 1, 2, 3], [4, 5, 6, 7]]

    with ExitStack() as es:
        cx = Ctx(nc, es)
        es.enter_context(nc.allow_non_contiguous_dma(reason="small params / strided outputs"))
        block = es.enter_context(nc.Block())
        cx.add_engine("pe", nc.tensor, in_order=True)
        cx.add_engine("act", nc.scalar)
        cx.add_engine("dve", nc.vector)
        cx.add_engine("pool", nc.gpsimd, ndma=8)
        cx.add_engine("sp", nc.sync, ndma=12)
        cc_cnt = [0]

        PS = Pool(cx, es, "ps", [128, 512], F32, int(os.environ.get("MK_NPS", "6")), psum=True)
        PSA = Pool(cx, es, "psa", [128, 512], F32, 2, psum=True) if os.environ.get("MK_NPS", "6") == "6" else PS
        CONST, R_const = single(cx, es, "CONST", [128, NCONST], F32)
        ONESB, R_onesb = single(cx, es, "ONESB", [128, 128], BF16)
        MASKDB, R_maskdb = single(cx, es, "MASKDB", [128, 2048], BF16)

        op, dma = cx.op, cx.dma

        def cst(c0, n, rows=128):
            return CONST[0:rows, c0:c0 + n]

        ident = cst(C_ID, 128)
        dma("sp", lambda h: h.dma_start(out=CONST[:, :], in_=consts_d), [R_in], [R_const])
        op("dve", lambda h: h.tensor_copy(ONESB[:, :], cst(C_ONES, 128)), [R_const], [R_onesb])
        op("dve", lambda h: h.tensor_copy(MASKDB[:, :], cst(C_MASKD, 2048)), [R_const], [R_maskdb])

        def mm(out, lhsT, rhs, reads, wres, start=True, stop=True):
            return op("pe", lambda h: h.matmul(out, lhsT, rhs, start=start, stop=stop), reads, [wres])

        def tp(out, in_, rows, reads, wres):
            return op("pe", lambda h: h.transpose(out, in_, CONST[0:rows, C_ID:C_ID + rows]), list(reads) + [R_const], [wres])

        def act(out, in_, func, reads, writes, bias=0.0, scale=1.0):
            if isinstance(bias, float):
                assert bias in (0.0, 1.0), bias
            else:
                reads = list(reads)
            return op("act", lambda h: h.activation(out, in_, func, bias=bias, scale=scale), reads, writes)

        def tt(en, out, in0, in1, alu, reads, writes):
            return op(en, lambda h: h.tensor_tensor(out, in0, in1, alu), reads, writes)

        def ts(en, out, in0, s1, op0, reads, writes, s2=None, op1=None):
            if op1 is None:
                return op(en, lambda h: h.tensor_scalar(out, in0, s1, None, op0), reads, writes)
            return op(en, lambda h: h.tensor_scalar(out, in0, s1, s2, op0, op1), reads, writes)

        def stt(en, out, in0, sc, in1, op0, op1, reads, writes):
            return op(en, lambda h: h.scalar_tensor_tensor(out, in0, sc, in1, op0, op1), reads, writes)

        def cp(en, out, in_, reads, writes):
            if en == "act":
                return op("act", lambda h: h.copy(out, in_), reads, writes)
            return op(en, lambda h: h.tensor_copy(out, in_), reads, writes)

        def recip(out, in_, reads, writes):
            return op("dve", lambda h: h.reciprocal(out, in_), reads, writes)

        def sigmoid_inplace(t_ap, src_ap, reads, res, neg=False):
            act(t_ap, src_ap, AF.Exp, reads, [res], scale=(1.0 if neg else -1.0))
            act(t_ap, t_ap, AF.Ln, [res], [res], bias=1.0)
            act(t_ap, t_ap, AF.Exp, [res], [res], scale=-1.0)

        def allgather(src, dst, rsrc, rdst, nchunks):
            e = cx.engs["pool"]
            cx._deps(e, [rsrc], [rdst])
            tok = None
            for k in range(nchunks):
                sem = es.enter_context(nc.semaphore("cc%d" % cc_cnt[0]))
                cc_cnt[0] += 1
                inst = e.h.collective_compute("AllGather", ALU.bypass, replica_groups=groups,
                                              ins=[src[k * 128:(k + 1) * 128, :]], outs=[dst[k]])
                inst.then_inc(sem, 1)
                tok = (sem, 1)
                cx._wait(e, tok)
            cx._record(tok, [rsrc], [rdst])

        halves = [(0, 1032, [(0, 512), (512, 512), (1024, 8)]), (1032, 1032, [(1032, 512), (1544, 504), (2048, 16)])]

        PRM = Pool(cx, es, "prm", [128, 64], F32, 1)
        prm, R_prm = PRM.get()
        LBB, R_lbb = single(cx, es, "LBB", [128, 256], F32)
        BGB, R_bgb = single(cx, es, "BGB", [128, 64], F32)
        WG2, R_wg2 = single(cx, es, "WG2", [16, 64], F32)

        def load_layer_params(l):
            dma("sp", lambda h: h.dma_start(out=prm[:, 0:8], in_=n1_d[l].rearrange("(c p) -> p c", p=128)), [R_in], [R_prm])
            dma("sp", lambda h: h.dma_start(out=prm[:, 8:16], in_=n2_d[l].rearrange("(c p) -> p c", p=128)), [R_in], [R_prm])
            nxt = n1_d[l + 1] if l == 0 else nf_d
            dma("sp", lambda h: h.dma_start(out=prm[:, 16:24], in_=nxt.rearrange("(c p) -> p c", p=128)), [R_in], [R_prm])
            dma("sp", lambda h: h.dma_start(out=prm[:, 24:25], in_=gng_d[l].rearrange("(p o) -> p o", o=1)), [R_in], [R_prm])
            dma("sp", lambda h: h.dma_start(out=prm[:, 25:26], in_=hng_d[l].rearrange("(p o) -> p o", o=1)), [R_in], [R_prm])
            dma("sp", lambda h: h.dma_start(out=prm[:, 26:27], in_=fbf_d[l].partition_broadcast(128)), [R_in], [R_prm])
            ts("dve", prm[:, 26:27], prm[:, 26:27], -1.0, ALU.mult, [R_prm], [R_prm])
            if l == 0:
                op("dve", lambda h: h.memset(prm[:, 27:28], 0.0), [], [R_prm])
            else:
                dma("sp", lambda h: h.dma_start(out=prm[:, 30:31], in_=hlb_d[0].rearrange("(p o) -> p o", o=1)), [R_in], [R_prm])
                dma("sp", lambda h: h.dma_start(out=prm[:, 31:32], in_=hlb_d[1].rearrange("(p o) -> p o", o=1)), [R_in], [R_prm])
                tt("dve", prm[:, 27:28], prm[:, 31:32], prm[:, 30:31], ALU.subtract, [R_prm], [R_prm])
                sigmoid_inplace(prm[:, 27:28], prm[:, 27:28], [R_prm], R_prm)
            ts("dve", prm[:, 28:29], prm[:, 27:28], -1.0, ALU.mult, [R_prm], [R_prm], s2=1.0, op1=ALU.add)
            dg_t, dg_r = TMPF.get()
            for j, col in enumerate((27, 28)):
                ts("dve", dg_t[:, 0:128], ident, prm[:, col:col + 1], ALU.mult, [R_prm, R_const], [dg_r])
                pt_, pr_ = PS.get()
                mm(pt_[:, 0:128], cst(C_ONES, 128), dg_t[:, 0:128], [R_const, dg_r], pr_)
                cp("dve", LBB[:, j * 128:(j + 1) * 128], pt_[:, 0:128], [pr_], [R_lbb])
            dma("sp", lambda h: h.dma_start(out=BGB[:, :], in_=bg_d[l].partition_broadcast(128)), [R_in], [R_bgb])
            dma("sp", lambda h: h.dma_start(out=WG2[:, :], in_=wg2_d[l]), [R_in], [R_wg2])

        TMPF = Pool(cx, es, "tmpf", [128, 512], F32, 4)

        def norm_half(xT, R_xT, W, tls, t0, gcol, dst_fn, rdst):
            for (g0, w) in tls:
                lo = g0 - t0
                sq_t, sq_r = SQ.get()
                op("dve", lambda h: h.tensor_tensor(sq_t[:, :, 0:w], xT[:, :, lo:lo + w], xT[:, :, lo:lo + w], ALU.mult), [R_xT], [sq_r])
                pt_, pr_ = PS.get()
                for c in range(8):
                    mm(pt_[:, 0:w], ONESB[:, :], sq_t[:, c, 0:w], [R_onesb, sq_r], pr_, start=(c == 0), stop=(c == 7))
                rs_t, rs_r = TMPF.get()
                act(rs_t[:, 0:w], pt_[:, 0:w], AF.Ln, [pr_, R_const], [rs_r], bias=CONST[:, C_EPS:C_EPS + 1], scale=1.0 / D)
                act(rs_t[:, 0:w], rs_t[:, 0:w], AF.Exp, [rs_r], [rs_r], scale=-0.5)
                for c in range(8):
                    stt("dve", dst_fn(c, lo, w), xT[:, c, lo:lo + w], prm[:, gcol + c:gcol + c + 1], rs_t[:, 0:w],
                        ALU.mult, ALU.mult, [R_xT, R_prm, rs_r], [rdst])

        wst_cur = [None]

        def load_w(dst_ap, src_ap, shape3, rdst):
            st_t, st_r = wst_cur[0].get()
            a, b = shape3
            view = st_t[:, 0:a * b].rearrange("p (a b) -> p a b", a=a)
            dma("sp", lambda h: h.dma_start(out=view, in_=src_ap), [R_in], [st_r])
            cp("pool", dst_ap, view, [st_r], [rdst])

        SQ = Pool(cx, es, "sq", [128, 8, 512], BF16, 1)

        def token_pre_layer0():
            with ExitStack() as ph:
                XT, R_XT = single(cx, ph, "XT0", [128, 8, 1040], F32)
                XN, R_XN = single(cx, ph, "XN0", [128, 8, 1040], BF16)
                XL = Pool(cx, ph, "xl", [128, D], F32, 2)
                for (t0, W, tls) in halves:
                    for (g0, w) in tls:
                        for u in range(0, w, 128):
                            n = min(128, w - u)
                            xl_t, xl_r = XL.get()
                            src = xp[g0 + u:g0 + u + n, :] if g0 < 2048 else xs[0:16, :]
                            dma("sp", lambda h: h.dma_start(out=xl_t[0:n, :], in_=src), [R_in], [xl_r])
                            for cc in range(0, 8, 4):
                                pt_, pr_ = PS.get()
                                for c in range(cc, cc + 4):
                                    tp(pt_[:, (c - cc) * 128:(c - cc) * 128 + n], xl_t[0:n, c * 128:(c + 1) * 128], n, [xl_r], pr_)
                                lo = g0 - t0 + u
                                cp("act", XT[:, cc:cc + 4, lo:lo + n],
                                   pt_[:, :].rearrange("p (c t) -> p c t", c=4)[:, :, 0:n], [pr_], [R_XT])
                    dma("pool", lambda h: h.dma_start(out=xres.rearrange("(c p) t -> p c t", p=128)[:, :, t0:t0 + W], in_=XT[:, :, 0:W]), [R_XT], [R_xres])
                    norm_half(XT, R_XT, W, tls, t0, 0, lambda c, lo, w: XN[:, c, lo:lo + w], R_XN)
                    dma("pool", lambda h: h.dma_start(out=a1.rearrange("(c p) t -> p c t", p=128)[:, :, t0:t0 + W], in_=XN[:, :, 0:W]), [R_XN], [R_a1])
                cx.barrier()

        def head_phase(l):
            with ExitStack() as ph:
                wst_cur[0] = Pool(cx, ph, "wsth", [128, 1024], F32, 2)
                WF, R_WF = single(cx, ph, "WF", [128, 8, NF], BF16)
                WT, R_WT = single(cx, ph, "WT", [128, 8, NT_], BF16)
                for c in range(8):
                    load_w(WF[:, c:c + 1, :], wF_d[l, c * 128:(c + 1) * 128, :].rearrange("p (a n) -> p a n", a=1), (1, NF), R_WF)
                    load_w(WT[:, c:c + 1, :], wT_d[l, c * 128:(c + 1) * 128, :].rearrange("p (a n) -> p a n", a=1), (1, NT_), R_WT)
                XNT = Pool(cx, ph, "xnt", [128, 8, 512], BF16, 2)
                ZF = {}
                for (nm, c0, wd) in FG:
                    ZF[nm] = single(cx, ph, "zf_" + nm, [128, 512], BF16 if nm in ("fq",) else F32)
                ZT, R_ZT = single(cx, ph, "ZT", [128, 4, NT_], F32)
                OT = Pool(cx, ph, "ot", [128, 3, 512], BF16, 2)
                ORAW, R_ORAW = single(cx, ph, "ORAW", [128, 512], F32)
                T128 = Pool(cx, ph, "t128", [128, 128], F32, 16)
                B128 = Pool(cx, ph, "b128", [128, 128], BF16, 16)
                TE = Pool(cx, ph, "te", [128, 260], F32, 4)
                GATE = Pool(cx, ph, "gate", [128, 512], F32, 3)
                ORAWH, R_ORAWH = single(cx, ph, "ORAWH", [128, 512], F32)
                PSG = SubPool(PS.tiles[0:3])
                PSH = SubPool(PS.tiles[3:6])
                SG, R_SG = single(cx, ph, "SG", [64, 128], F32)
                SGB, R_SGB = single(cx, ph, "SGB", [64, 128], BF16)
                SH, R_SH = single(cx, ph, "SH", [128, 128], F32)
                SHB, R_SHB = single(cx, ph, "SHB", [128, 128], BF16)

                def fproj(xn_t, xn_r, w):
                    for (nm, c0, wd) in FG:
                        pt_, pr_ = PS.get()
                        for c in range(8):
                            mm(pt_[0:wd, 0:w], WF[:, c, c0:c0 + wd], xn_t[:, c, 0:w], [R_WF, xn_r], pr_, start=(c == 0), stop=(c == 7))
                        z_t, z_r = ZF[nm]
                        cp("act", z_t[0:wd, 0:w], pt_[0:wd, 0:w], [pr_], [z_r])

                def tproj(xn_t, xn_r, col0, n, u):
                    for (a, b) in ((0, TA), (TA, NT_)):
                        pt_, pr_ = PS.get()
                        for c in range(8):
                            mm(pt_[0:n, 0:b - a], xn_t[:, c, col0:col0 + n], WT[:, c, a:b], [R_WT, xn_r], pr_, start=(c == 0), stop=(c == 7))
                        cp("dve", ZT[0:n, u, a:b], pt_[0:n, 0:b - a], [pr_], [R_ZT])

                def logsig(out_ap, in_ap, n, reads, wres, bias_ap):
                    act(out_ap, in_ap, AF.Exp, reads, [wres], bias=bias_ap, scale=-1.0)
                    act(out_ap, out_ap, AF.Ln, [wres], [wres], bias=1.0)
                    ts("dve", out_ap, out_ap, -1.0, ALU.mult, [wres], [wres])

                def scan_step(n, K, qT, kT, rs_qk, k_tok, v_tok, la, rs_tok, S, R_S, SB, R_SB, lnscale, out_ps_fn, PS=PS):
                    if n == 128:
                        ccat, ncat, csuf, cmask, nch, C = C_CAT2, 258, C_SUF2, C_MASK2, 2, 64
                    else:
                        ccat, ncat, csuf, cmask, nch, C = C_CAT4, 9, C_SUF4, C_MASK4, 1, 4
                    pb_t, pb_r = PS.get()
                    mm(pb_t[0:K, 0:ncat], la, cst(ccat, ncat, n), rs_tok + [R_const], pb_r)
                    psuf_t, psuf_r = PS.get()
                    mm(psuf_t[0:n, 0:K], cst(csuf, n, n), la, rs_tok + [R_const], psuf_r)
                    e_t, e_r = TE.get()
                    lb_ = 0.0 if lnscale is None else CONST[0:K, lnscale:lnscale + 1]
                    act(e_t[0:K, 0:n], pb_t[0:K, 0:n], AF.Exp, [pb_r, R_const], [e_r], bias=lb_)
                    act(e_t[0:K, 128:128 + n], pb_t[0:K, n:2 * n], AF.Exp, [pb_r, R_const], [e_r], bias=lb_)
                    ek_t, ek_r = T128.get()
                    act(ek_t[0:K, 0:n], pb_t[0:K, n:2 * n], AF.Exp, [pb_r], [ek_r], scale=-1.0)
                    act(e_t[0:K, 256:256 + nch], pb_t[0:K, 2 * n:2 * n + nch], AF.Exp, [pb_r], [e_r])
                    es_t, es_r = T128.get()
                    act(es_t[0:n, 0:K], psuf_t[0:n, 0:K], AF.Exp, [psuf_r], [es_r])
                    qb_t, qb_r = B128.get()
                    tt("dve", qb_t[0:K, 0:n], qT, e_t[0:K, 0:n], ALU.mult, rs_qk + [e_r], [qb_r])
                    qr_t, qr_r = B128.get()
                    tt("dve", qr_t[0:K, 0:n], qT, e_t[0:K, 128:128 + n], ALU.mult, rs_qk + [e_r], [qr_r])
                    kr_t, kr_r = B128.get()
                    tt("dve", kr_t[0:K, 0:n], kT, ek_t[0:K, 0:n], ALU.mult, rs_qk + [ek_r], [kr_r])
                    kh_t, kh_r = B128.get()
                    tt("dve", kh_t[0:n, 0:K], k_tok, es_t[0:n, 0:K], ALU.mult, rs_tok + [es_r], [kh_r])
                    vb_t, vb_r = B128.get()
                    cp("act", vb_t[0:n, 0:128], v_tok, rs_tok, [vb_r])
                    yield
                    pa_t, pa_r = PS.get()
                    mm(pa_t[0:n, 0:n], kr_t[0:K, 0:n], qr_t[0:K, 0:n], [kr_r, qr_r], pa_r)
                    am_t, am_r = B128.get()
                    tt("dve", am_t[0:n, 0:n], pa_t[0:n, 0:n], cst(cmask, n, n), ALU.mult, [pa_r, R_const], [am_r])
                    yield
                    po_t, po_r = PS.get()
                    mm(po_t[:, 0:n], vb_t[0:n, 0:128], am_t[0:n, 0:n], [vb_r, am_r], po_r, start=True, stop=False)
                    for ci in range(nch):
                        mm(po_t[:, ci * C:(ci + 1) * C], SB[0:K, :], qb_t[0:K, ci * C:(ci + 1) * C], [R_SB, qb_r], po_r,
                           start=False, stop=(ci == nch - 1))
                        pS_t, pS_r = PS.get()
                        mm(pS_t[0:K, 0:128], kh_t[ci * C:ci * C + C, 0:K], vb_t[ci * C:ci * C + C, 0:128], [kh_r, vb_r], pS_r)
                        stt("dve", S[0:K, :], S[0:K, :], e_t[0:K, 256 + ci:257 + ci], pS_t[0:K, 0:128], ALU.mult, ALU.add,
                            [R_S, e_r, pS_r], [R_S])
                        cp("dve", SB[0:K, :], S[0:K, :], [R_S], [R_SB])
                        yield
                    out_ps_fn(po_t, po_r)

                def finish_o(w, nrm_col, gate_ap, gate_r, ot_t, ot_r, x, ORAW=ORAW, R_ORAW=R_ORAW, PS=PS):
                    sq_t, sq_r = B512.get()
                    tt("dve", sq_t[:, 0:w], ORAW[:, 0:w], ORAW[:, 0:w], ALU.mult, [R_ORAW], [sq_r])
                    pt_, pr_ = PS.get()
                    mm(pt_[:, 0:w], ONESB[:, :], sq_t[:, 0:w], [R_onesb, sq_r], pr_)
                    rs_t, rs_r = TMPF.get()
                    act(rs_t[:, 0:w], pt_[:, 0:w], AF.Ln, [pr_, R_const], [rs_r], bias=CONST[:, C_EPS:C_EPS + 1], scale=1.0 / 128)
                    act(rs_t[:, 0:w], rs_t[:, 0:w], AF.Exp, [rs_r], [rs_r], scale=-0.5)
                    stt("dve", rs_t[:, 0:w], ORAW[:, 0:w], prm[:, nrm_col:nrm_col + 1], rs_t[:, 0:w], ALU.mult, ALU.mult,
                        [R_ORAW, R_prm, rs_r], [rs_r])
                    tt("dve", ot_t[:, x, 0:w], rs_t[:, 0:w], gate_ap, ALU.mult, [rs_r, gate_r], [ot_r])

                def silu_gate(src_ap, src_r, w):
                    g_t, g_r = GATE.get()
                    sigmoid_inplace(g_t[:, 0:w], src_ap, [src_r], g_r)
                    tt("dve", g_t[:, 0:w], g_t[:, 0:w], src_ap, ALU.mult, [g_r, src_r], [g_r])
                    return g_t, g_r

                B512 = Pool(cx, ph, "b512", [128, 512], BF16, 4)
                scale = 128 ** -0.5

                def prep_hgrn_F(w):
                    hq_t, hq_r = ZF["hq"]
                    g_t, g_r = silu_gate(hq_t[:, 0:w], hq_r, w)
                    cp("dve", hq_t[:, 0:w], g_t[:, 0:w], [g_r], [hq_r])
                    hf_t, hf_r = ZF["hf"]
                    sigmoid_inplace(hf_t[:, 0:w], hf_t[:, 0:w], [hf_r], hf_r, neg=True)
                    ts("dve", hf_t[:, 0:w], hf_t[:, 0:w], prm[:, 28:29], ALU.mult, [hf_r, R_prm], [hf_r])

                def gla_tok(n, u, col0, PS=PS):
                    glr_t, glr_r = ZF["glr"]
                    pt_, pr_ = PS.get()
                    mm(pt_[0:n, 0:64], glr_t[0:16, col0:col0 + n], WG2[:, :], [glr_r, R_wg2], pr_)
                    la_t, la_r = T128.get()
                    tt("dve", la_t[0:n, 0:64], pt_[0:n, 0:64], BGB[0:n, :], ALU.add, [pr_, R_bgb], [la_r])
                    act(la_t[0:n, 0:64], la_t[0:n, 0:64], AF.Exp, [la_r], [la_r], scale=-1.0)
                    act(la_t[0:n, 0:64], la_t[0:n, 0:64], AF.Ln, [la_r], [la_r], bias=1.0)
                    ts("dve", la_t[0:n, 0:64], la_t[0:n, 0:64], -1.0 / 16.0, ALU.mult, [la_r], [la_r])
                    return la_t, la_r

                def hgrn_tok(n, u):
                    sg_t, sg_r = T128.get()
                    sigmoid_inplace(sg_t[0:n, :], ZT[0:n, u, 449:577], [R_ZT], sg_r)
                    t1_t, t1_r = T128.get()
                    tt("dve", t1_t[0:n, :], sg_t[0:n, :], LBB[0:n, 128:256], ALU.mult, [sg_r, R_lbb], [t1_r])
                    la_t, la_r = T128.get()
                    tt("dve", la_t[0:n, :], t1_t[0:n, :], LBB[0:n, 0:128], ALU.add, [t1_r, R_lbb], [la_r])
                    act(la_t[0:n, :], la_t[0:n, :], AF.Ln, [la_r], [la_r])
                    kt_t, kt_r = T128.get()
                    tt("dve", kt_t[0:n, :], LBB[0:n, 128:256], t1_t[0:n, :], ALU.subtract, [t1_r, R_lbb], [kt_r])
                    return la_t, la_r, kt_t, kt_r

                def sample_sweep(sm):
                    KGP = Pool(cx, sm, "kg", [128, 512], F32, 8)
                    VGP = Pool(cx, sm, "vg", [128, 512], F32, 8)
                    KTS, R_KTS = single(cx, sm, "KTS", [128, 16 * 128], BF16)
                    PTQ, R_PTQ = single(cx, sm, "PTQ", [128, 16, 16], I32)
                    PTQF, R_PTQF = single(cx, sm, "PTQF", [128, 16, 16], F32)
                    IDX, R_IDX = single(cx, sm, "IDX", [128, 16, 16], I32)
                    LFG, R_LFG = single(cx, sm, "LFG", [128, 16, 4], F32)
                    RB, R_RB = single(cx, sm, "RB", [128, 16, 4], F32)
                    R1, R_R1 = single(cx, sm, "R1", [128, 16, 4], F32)
                    RS, R_RS = single(cx, sm, "RS", [128, 16], F32)
                    SBS, R_SBS = single(cx, sm, "SBS", [128, 16, 4], F32)
                    PTS, R_PTS = single(cx, sm, "PTS", [128, 64, 4], F32)
                    ORAW2, R_ORAW2 = single(cx, sm, "ORAW2", [128, 16], F32)
                    SM4 = Pool(cx, sm, "sm4", [128, 132], F32, 8)
                    FKB, R_FKB = single(cx, sm, "FKB", [128, 16], BF16)
                    for g4 in range(4):
                        dma("sp", lambda h: h.dma_start(out=PTQ[32 * g4:32 * g4 + 32, :, :], in_=pt_d[:, g4::4].partition_broadcast(32)), [R_in], [R_PTQ])
                    cp("dve", PTQF[:, :, :], PTQ[:, :, :], [R_PTQ], [R_PTQF])
                    ts("dve", PTQF[:, :, :], PTQF[:, :, :], 32.0, ALU.mult, [R_PTQF, R_const], [R_PTQF], s2=CONST[:, C_PM32:C_PM32 + 1], op1=ALU.add)
                    ts("dve", PTQF[:, :, :], PTQF[:, :, :], float(l * NPOOL * 32), ALU.add, [R_PTQF], [R_PTQF])
                    cp("dve", IDX[:, :, :], PTQF[:, :, :], [R_PTQF], [R_IDX])

                    def issue_gather(gb, q):
                        kgs = [KGP.get() for _ in range(4)]
                        vgs = [VGP.get() for _ in range(4)]
                        for (src_d, lst) in ((ck_d, kgs), (cv_d, vgs)):
                            for Gi in range(4):
                                G = q * 4 + Gi
                                dst, rdst = lst[Gi]
                                dma("pool", lambda h: h.indirect_dma_start(out=dst[:, :], out_offset=None, in_=src_d,
                                                                           in_offset=bass.IndirectOffsetOnAxis(ap=IDX[:, gb, G:G + 1], axis=0)),
                                    [R_in, R_IDX], [rdst])
                        return kgs, vgs
                    for s in range(4):
                        xn_t, xn_r = XNT.get()
                        dma("sp", lambda h: h.dma_start(out=xn_t[:, :, 0:16], in_=b1[:, s * 128:(s + 1) * 128, 2048:2064].rearrange("c p t -> p c t")), [R_b1], [xn_r])
                        fproj(xn_t, xn_r, 16)
                        ot_t, ot_r = OT.get()
                        prep_hgrn_F(16)
                        gq_t, gq_r = ZF["gq"]
                        gk_t, gk_r = ZF["gk"]
                        hq_t, hq_r = ZF["hq"]
                        hf_t, hf_r = ZF["hf"]
                        fq_t, fq_r = ZF["fq"]
                        fk_t, fk_r = ZF["fk"]
                        cp("dve", FKB[:, 0:16], fk_t[:, 0:16], [fk_r], [R_FKB])
                        pending = issue_gather(4 * s, 0)
                        for bb in range(4):
                            gb = 4 * s + bb
                            c0 = 4 * bb
                            tproj(xn_t, xn_r, c0, 4, 0)
                            dma("pool", lambda h: h.dma_start(out=fk_s[l, gb], in_=ZT[0:4, 0, 192:320]), [R_ZT], [Res("o")])
                            dma("pool", lambda h: h.dma_start(out=fv_s[l, gb], in_=ZT[0:4, 0, 320:448]), [R_ZT], [Res("o")])
                            lfn_t, lfn_r = SM4.get()
                            logsig(lfn_t[0:4, 0:1], ZT[0:4, 0, 448:449], 4, [R_ZT, R_prm], lfn_r, prm[0:4, 26:27])
                            dma("pool", lambda h: h.dma_start(out=lf_s[l, gb].rearrange("(t o) -> t o", o=1), in_=lfn_t[0:4, 0:1]), [lfn_r], [Res("o")])
                            dma("sp", lambda h: h.dma_start(out=SG[:, :], in_=sgla_d[l, gb]), [R_in], [R_SG])
                            cp("dve", SGB[:, :], SG[:, :], [R_SG], [R_SGB])
                            la_t, la_r = gla_tok(4, 0, c0)

                            def outfa(po_t, po_r, c0=c0):
                                cp("act", ORAW[:, c0:c0 + 4], po_t[:, 0:4], [po_r], [R_ORAW])
                            for _ in scan_step(4, 64, gq_t[0:64, c0:c0 + 4], gk_t[0:64, c0:c0 + 4], [gq_r, gk_r],
                                               ZT[0:4, 0, 0:64], ZT[0:4, 0, 64:192], la_t[0:4, 0:64], [R_ZT, la_r],
                                               SG, R_SG, SGB, R_SGB, C_LN8, outfa):
                                pass
                            dma("pool", lambda h: h.dma_start(out=gla_s[l, gb], in_=SG[:, :]), [R_SG], [Res("o")])
                            dma("sp", lambda h: h.dma_start(out=SH[:, :], in_=shg_d[l, gb]), [R_in], [R_SH])
                            cp("dve", SHB[:, :], SH[:, :], [R_SH], [R_SHB])
                            la_t, la_r, kt_t, kt_r = hgrn_tok(4, 0)

                            def outfc(po_t, po_r, c0=c0):
                                cp("act", ORAW2[:, c0:c0 + 4], po_t[:, 0:4], [po_r], [R_ORAW2])
                            for _ in scan_step(4, 128, hq_t[:, c0:c0 + 4], hf_t[:, c0:c0 + 4], [hq_r, hf_r],
                                               kt_t[0:4, :], ZT[0:4, 0, 577:705], la_t[0:4, :], [R_ZT, la_r, kt_r],
                                               SH, R_SH, SHB, R_SHB, None, outfc):
                                pass
                            dma("pool", lambda h: h.dma_start(out=hg_s[l, gb], in_=SH[:, :]), [R_SH], [Res("o")])
                            lfg_rs = [Res("lfg") for _ in range(16)]
                            for G in range(16):
                                dma("pool", lambda h: h.indirect_dma_start(out=LFG[:, G, :], out_offset=None, in_=clf_d,
                                                                           in_offset=bass.IndirectOffsetOnAxis(ap=IDX[:, gb, G:G + 1], axis=0)),
                                    [R_in, R_IDX, R_LFG], [lfg_rs[G]])
                            R_LFGS = lfg_rs + [R_LFG]
                            op("dve", lambda h: h.tensor_reduce(RS[:, :], LFG[:, :, :], AX.X, ALU.add), R_LFGS, [R_RS])
                            op("dve", lambda h: h.memset(R1[:, :, 3:4], 0.0), [], [R_R1])
                            cp("dve", R1[:, :, 2:3], LFG[:, :, 3:4], R_LFGS, [R_R1])
                            tt("dve", R1[:, :, 1:2], LFG[:, :, 2:3], LFG[:, :, 3:4], ALU.add, R_LFGS, [R_R1])
                            tt("dve", R1[:, :, 0:1], R1[:, :, 1:2], LFG[:, :, 1:2], ALU.add, R_LFGS + [R_R1], [R_R1, R_LFG])
                            prt_t, prt_r = PS.get()
                            tp(prt_t[0:16, 0:128], RS[:, :], 128, [R_RS], prt_r)
                            ct_t, ct_r = SM4.get()
                            op("dve", lambda h: h.tensor_reduce(ct_t[0:16, 128:129], prt_t[0:16, 0:128], AX.X, ALU.add), [prt_r], [ct_r])
                            ts("dve", ct_t[0:16, 0:128], cst(C_ONES, 128, 16), ct_t[0:16, 128:129], ALU.mult, [ct_r, R_const], [ct_r])
                            pr2_t, pr2_r = PS.get()
                            mm(pr2_t[:, 0:16], cst(C_U128, 128), RS[:, :], [R_const, R_RS], pr2_r, start=True, stop=False)
                            mm(pr2_t[:, 0:16], ct_t[0:16, 0:128], cst(C_WG, 16, 16), [ct_r, R_const], pr2_r, start=False, stop=True)
                            cp("dve", RS[:, :], pr2_t[:, 0:16], [pr2_r], [R_RS])
                            tt("dve", RB[:, :, :], R1[:, :, :], RS[:, :].unsqueeze(2).to_broadcast([128, 16, 4]), ALU.add, [R_R1, R_RS], [R_RB])
                            po_t, po_r = PSA.get()
                            pd_t, pd_r = PSA.get()
                            for q in range(4):
                                kgs, vgs = pending
                                nxt = (gb, q + 1) if q < 3 else ((gb + 1, 0) if bb < 3 else None)
                                if nxt is not None:
                                    pending = issue_gather(*nxt)
                                for m4 in range(4):
                                    ptr_t, ptr_r = PS.get()
                                    for k in range(4):
                                        tp(ptr_t[:, k * 128:(k + 1) * 128], kgs[m4][0][:, k * 128:(k + 1) * 128], 128, [kgs[m4][1]], ptr_r)
                                    cp("act" if m4 % 2 == 0 else "dve", KTS[:, m4 * 512:(m4 + 1) * 512], ptr_t[:, :], [ptr_r], [R_KTS])
                                st_t, st_r = PS.get()
                                for m in range(16):
                                    mm(st_t[:, m * 4:(m + 1) * 4], KTS[:, m * 128:(m + 1) * 128], fq_t[:, c0:c0 + 4], [R_KTS, fq_r], st_r)
                                stt("dve", SBS[:, :, :], st_t[:, 0:64].rearrange("p (m q) -> p m q", q=4), scale,
                                    RB[:, q * 4:(q + 1) * 4, :].rearrange("p g t -> p (g t)").unsqueeze(2).to_broadcast([128, 16, 4]),
                                    ALU.mult, ALU.add, [st_r, R_RB], [R_SBS])
                                act(PTS[:, q * 16:(q + 1) * 16, :], SBS[:, :, :], AF.Exp, [R_SBS], [R_PTS])
                                for m in range(16):
                                    mm(po_t[0:4, 0:128], PTS[:, q * 16 + m, :], vgs[m // 4][0][:, (m % 4) * 128:(m % 4 + 1) * 128], [R_PTS, vgs[m // 4][1]], po_r,
                                       start=(q == 0 and m == 0), stop=False)
                            psn_t, psn_r = PS.get()
                            mm(psn_t[0:4, 0:4], FKB[:, c0:c0 + 4], fq_t[:, c0:c0 + 4], [R_FKB, fq_r], psn_r)
                            mm(psn_t[0:4, 8:9], cst(C_CAT4, 4, 4), lfn_t[0:4, 0:1], [R_const, lfn_r], psn_r)
                            nc_t, nc_r = SM4.get()
                            ts("dve", nc_t[0:4, 0:1], psn_t[0:4, 8:9], -1.0, ALU.mult, [psn_r], [nc_r])
                            act(nc_t[0:4, 4:8], psn_t[0:4, 0:4], AF.Exp, [psn_r, nc_r], [nc_r], bias=nc_t[0:4, 0:1], scale=scale)
                            tt("dve", nc_t[0:4, 4:8], nc_t[0:4, 4:8], cst(C_MASK4, 4, 4), ALU.mult, [nc_r, R_const], [nc_r])
                            mm(po_t[0:4, 0:128], nc_t[0:4, 4:8], ZT[0:4, 0, 320:448], [nc_r, R_ZT], po_r, start=False, stop=True)
                            ps_t, ps_r = SM4.get()
                            op("dve", lambda h: h.tensor_reduce(ps_t[:, 0:4], PTS[:, :, :].rearrange("p m q -> p q m"), AX.X, ALU.add), [R_PTS], [ps_r])
                            mm(pd_t[0:4, 0:1], ps_t[:, 0:4], cst(C_ONES, 1), [ps_r, R_const], pd_r, start=True, stop=False)
                            mm(pd_t[0:4, 0:1], nc_t[0:4, 4:8], cst(C_ONES, 1, 4), [nc_r, R_const], pd_r, start=False, stop=True)
                            ob_t, ob_r = SM4.get()
                            recip(ob_t[0:4, 128:129], pd_t[0:4, 0:1], [pd_r], [ob_r])
                            ts("dve", ob_t[0:4, 0:128], po_t[0:4, 0:128], ob_t[0:4, 128:129], ALU.mult, [po_r, ob_r], [ob_r])
                            pot_t, pot_r = PS.get()
                            tp(pot_t[:, 0:4], ob_t[0:4, 0:128], 4, [ob_r], pot_r)
                            cp("dve", ot_t[:, 1, c0:c0 + 4], pot_t[:, 0:4], [pot_r], [ot_r])
                        gr_t, gr_r = ZF["gr"]
                        g_t, g_r = silu_gate(gr_t[:, 0:16], gr_r, 16)
                        finish_o(16, 24, g_t[:, 0:16], g_r, ot_t, ot_r, 0)
                        hg_t, hg_r = ZF["hg"]
                        g_t, g_r = silu_gate(hg_t[:, 0:16], hg_r, 16)
                        finish_o(16, 25, g_t[:, 0:16], g_r, ot_t, ot_r, 2, ORAW=ORAW2, R_ORAW=R_ORAW2)
                        dma("pool", lambda h: h.dma_start(out=a2[s * 384:(s + 1) * 384, 2048:2064].rearrange("(x p) t -> p x t", p=128), in_=ot_t[:, :, 0:16]),
                            [ot_r], [Res("a2s")])

                pp = ExitStack()
                KT, R_KT = single(cx, pp, "KT", [128, SEQ], BF16)
                VP, R_VP = single(cx, pp, "VP", [128, 64, 128], BF16)
                LF, R_LF = single(cx, pp, "LF", [128, 64], F32)
                NEGC, R_NEGC = single(cx, pp, "NEGC", [128, 64], F32)
                NB, R_NB = single(cx, pp, "NB", [128, 64], F32)
                CAR, R_CAR = single(cx, pp, "CAR", [128, 2], F32)
                PT = Pool(cx, pp, "pt", [128, 512], BF16, 3)
                op("dve", lambda h: h.memset(SG[:, :], 0.0), [], [R_SG])
                op("dve", lambda h: h.memset(SGB[:, :], 0.0), [], [R_SGB])
                op("dve", lambda h: h.memset(SH[:, :], 0.0), [], [R_SH])
                op("dve", lambda h: h.memset(SHB[:, :], 0.0), [], [R_SHB])
                op("dve", lambda h: h.memset(CAR[:, :], 0.0), [], [R_CAR])
                for s in range(4):
                    for i in range(4):
                        ti = s * 4 + i
                        g0 = s * 2048 + i * 512
                        xn_t, xn_r = XNT.get()
                        dma("sp", lambda h: h.dma_start(out=xn_t[:, :, :], in_=b1[:, s * 128:(s + 1) * 128, i * 512:(i + 1) * 512].rearrange("c p t -> p c t")),
                            [R_b1], [xn_r])
                        fproj(xn_t, xn_r, 512)
                        for u in range(4):
                            tproj(xn_t, xn_r, u * 128, 128, u)
                        if DEBUG and l == 0 and ti == 0:
                            dma("sp", lambda h: h.dma_start(out=dbg_xn, in_=xn_t[:, :, :]), [xn_r], [Res("o")])
                            dma("sp", lambda h: h.dma_start(out=dbg_wt, in_=WT[:, :, :]), [R_WT], [Res("o")])
                            dma("sp", lambda h: h.dma_start(out=dbg_zt, in_=ZT[:, :, :]), [R_ZT], [Res("o")])
                        fk_t, fk_r = ZF["fk"]
                        cp("pool", KT[:, g0:g0 + 512], fk_t[:, 0:512], [fk_r], [R_KT])
                        cp("pool", VP[:, ti * 4:ti * 4 + 4, :], ZT[:, :, 320:448], [R_ZT], [R_VP])
                        dma("pool", lambda h: h.dma_start(out=fk_p[l, g0:g0 + 512, :].rearrange("(u p) d -> p u d", p=128), in_=ZT[:, :, 192:320]), [R_ZT], [Res("o")])
                        dma("pool", lambda h: h.dma_start(out=fv_p[l, g0:g0 + 512, :].rearrange("(u p) d -> p u d", p=128), in_=ZT[:, :, 320:448]), [R_ZT], [Res("o")])
                        logsig(LF[:, ti * 4:ti * 4 + 4], ZT[:, :, 448], 128, [R_ZT], R_LF, prm[:, 26:27])
                        pc_t, pc_r = PS.get()
                        mm(pc_t[:, 0:4], cst(C_INC128, 128), LF[:, ti * 4:ti * 4 + 4], [R_const, R_LF], pc_r)
                        mm(pc_t[:, 4:8], cst(C_ONES, 128), LF[:, ti * 4:ti * 4 + 4], [R_const, R_LF], pc_r)
                        cp("dve", CAR[:, 1:2], CAR[:, 0:1], [R_CAR], [R_CAR])
                        for k in range(4):
                            blk = ti * 4 + k
                            stt("dve", NEGC[:, blk:blk + 1], pc_t[:, k:k + 1], -1.0, CAR[:, 0:1], ALU.mult, ALU.subtract,
                                [pc_r, R_CAR], [R_NEGC])
                            tt("dve", CAR[:, 0:1], CAR[:, 0:1], pc_t[:, 4 + k:5 + k], ALU.add, [R_CAR, pc_r], [R_CAR])
                        nblk = ti * 4 + 4
                        ts("dve", NB[:, 0:nblk], NEGC[:, 0:nblk], CAR[:, 1:2], ALU.add, [R_NEGC, R_CAR], [R_NB])
                        ot_t, ot_r = OT.get()
                        gq_t, gq_r = ZF["gq"]
                        gk_t, gk_r = ZF["gk"]
                        hq_t, hq_r = ZF["hq"]
                        hf_t, hf_r = ZF["hf"]

                        def gen_gla():
                            for u in range(4):
                                la_t, la_r = gla_tok(128, u, u * 128, PSG)
                                yield

                                def outfn(po_t, po_r, u=u):
                                    cp("act", ORAW[:, u * 128:(u + 1) * 128], po_t[:, 0:128], [po_r], [R_ORAW])
                                yield from scan_step(128, 64, gq_t[0:64, u * 128:(u + 1) * 128], gk_t[0:64, u * 128:(u + 1) * 128], [gq_r, gk_r],
                                                     ZT[:, u, 0:64], ZT[:, u, 64:192], la_t[:, 0:64], [R_ZT, la_r],
                                                     SG, R_SG, SGB, R_SGB, C_LN8, outfn, PS=PSG)
                            gr_t, gr_r = ZF["gr"]
                            g_t, g_r = silu_gate(gr_t[:, 0:512], gr_r, 512)
                            yield
                            finish_o(512, 24, g_t[:, 0:512], g_r, ot_t, ot_r, 0, PS=PSG)

                        def gen_hgrn():
                            prep_hgrn_F(512)
                            yield
                            for u in range(4):
                                la_t, la_r, kt_t, kt_r = hgrn_tok(128, u)
                                yield

                                def outfn(po_t, po_r, u=u):
                                    cp("act", ORAWH[:, u * 128:(u + 1) * 128], po_t[:, 0:128], [po_r], [R_ORAWH])
                                yield from scan_step(128, 128, hq_t[:, u * 128:(u + 1) * 128], hf_t[:, u * 128:(u + 1) * 128], [hq_r, hf_r],
                                                     kt_t[:, :], ZT[:, u, 577:705], la_t[:, :], [R_ZT, la_r, kt_r],
                                                     SH, R_SH, SHB, R_SHB, None, outfn, PS=PSH)
                            hg_t, hg_r = ZF["hg"]
                            g_t, g_r = silu_gate(hg_t[:, 0:512], hg_r, 512)
                            yield
                            finish_o(512, 25, g_t[:, 0:512], g_r, ot_t, ot_r, 2, ORAW=ORAWH, R_ORAW=R_ORAWH, PS=PSH)

                        gens = [gen_gla(), gen_hgrn()]
                        if os.environ.get('MK_SEQ', '0') == '1':
                            for g_ in gens:
                                for _ in g_:
                                    pass
                            gens = []
                        while gens:
                            for g_ in list(gens):
                                try:
                                    next(g_)
                                except StopIteration:
                                    gens.remove(g_)
                        fq_t, fq_r = ZF["fq"]
                        pacc_t, pacc_r = PSA.get()
                        pden_t, pden_r = PSA.get()
                        nxt_ps = PS.get()
                        mm(nxt_ps[0][:, :], KT[:, 0:128], fq_t[:, 0:512], [R_KT, fq_r], nxt_ps[1])
                        for j in range(nblk):
                            pst_t, pst_r = nxt_ps
                            if j + 1 < nblk:
                                nxt_ps = PS.get()
                                mm(nxt_ps[0][:, :], KT[:, (j + 1) * 128:(j + 2) * 128], fq_t[:, 0:512], [R_KT, fq_r], nxt_ps[1])
                            p_t, p_r = PT.get()
                            act(p_t[:, :], pst_t[:, :], AF.Exp, [pst_r, R_NB], [p_r], bias=NB[:, j:j + 1], scale=scale)
                            if j >= ti * 4:
                                r = j - ti * 4
                                tt("pool", p_t[:, :], p_t[:, :], MASKDB[:, r * 512:(r + 1) * 512], ALU.mult, [p_r, R_maskdb], [p_r])
                            mm(pacc_t[:, :], VP[:, j, :], p_t[:, :], [R_VP, p_r], pacc_r, start=(j == 0), stop=(j == nblk - 1))
                            mm(pden_t[:, :], ONESB[:, :], p_t[:, :], [R_onesb, p_r], pden_r, start=(j == 0), stop=(j == nblk - 1))
                        rd_t, rd_r = TMPF.get()
                        recip(rd_t[:, :], pden_t[:, :], [pden_r], [rd_r])
                        tt("dve", ot_t[:, 1, :], pacc_t[:, :], rd_t[:, :], ALU.mult, [pacc_r, rd_r], [ot_r])
                        dma("pool", lambda h: h.dma_start(out=a2[s * 384:(s + 1) * 384, i * 512:(i + 1) * 512].rearrange("(x p) t -> p x t", p=128), in_=ot_t[:, :, :]),
                            [ot_r], [Res("a2s")])
                dma("pool", lambda h: h.dma_start(out=gla_p[l], in_=SG[:, :]), [R_SG], [Res("o")])
                dma("pool", lambda h: h.dma_start(out=hg_p[l], in_=SH[:, :]), [R_SH], [Res("o")])
                plf_t, plf_r = PS.get()
                tp(plf_t[0:64, 0:128], LF[:, :], 128, [R_LF], plf_r)
                lfo_t, lfo_r = T128.get()
                cp("dve", lfo_t[0:64, :], plf_t[0:64, 0:128], [plf_r], [lfo_r])
                dma("pool", lambda h: h.dma_start(out=lf_p[l], in_=lfo_t[0:64, :]), [lfo_r], [Res("o")])
                cx.barrier()
                pp.close()
                if os.environ.get('MK_NOSAMPLE', '0') != '1':
                    with ExitStack() as sm:
                        sample_sweep(sm)
                        cx.barrier()

        RIDX, R_ridx = single(cx, es, "RIDX", [128, 24], I32)
        dma("sp", lambda h: h.dma_start(out=RIDX[:, :], in_=ridx_d), [R_in], [R_ridx])

        PS_ALL = PS

        def token_post(l, last):
            with ExitStack() as ph:
                XT, R_XT = single(cx, ph, "XT", [128, 8, 1040], F32)
                MT, R_MT = single(cx, ph, "MT", [128, 8, 1040], BF16)
                HT, R_HT = single(cx, ph, "HT", [128, 8, 1040], BF16)
                UB, _ = single(cx, ph, "UB", [128, 22, 1040], BF16)
                R_OTG, R_XN, R_ACT = Res("otg"), Res("xn"), Res("act")
                WB = Pool(cx, ph, "wb", [128, 3072], BF16, 4)
                PS = SubPool(PS_ALL.tiles + PSA.tiles)
                wst_cur[0] = Pool(cx, ph, "wstt", [128, 2816], F32, 3)
                YL = Pool(cx, ph, "yl", [128, D], F32, 1)
                for (t0, W, tls) in halves:
                    dma("sp", lambda h: h.dma_start(out=XT[:, :, 0:W], in_=xres.rearrange("(c p) t -> p c t", p=128)[:, :, t0:t0 + W]), [R_xres], [R_XT])
                    dma("sp", lambda h: h.dma_start(out=UB[:, 12:20, 0:W], in_=a1.rearrange("(c p) t -> p c t", p=128)[:, :, t0:t0 + W]), [R_a1], [R_XN])
                    hv = 0 if t0 == 0 else 1
                    otg_rs = [Res("otg") for _ in range(12)]
                    for s_ in range(4):
                        for x in range(3):
                            col = hv * 12 + s_ * 3 + x
                            dma("pool", lambda h: h.indirect_dma_start(out=UB[:, x * 4 + s_, 0:W], out_offset=None, in_=b2.rearrange("q r (two t) -> (q r two) t", two=2),
                                                                       in_offset=bass.IndirectOffsetOnAxis(ap=RIDX[:, col:col + 1], axis=0)),
                                [R_b2, R_ridx, R_OTG], [otg_rs[x * 4 + s_]])
                    if DEBUG and l == 0:
                        dma("sp", lambda h: h.dma_start(out=dbg_o[:, :, t0:t0 + W], in_=UB[:, 0:12, 0:W]), otg_rs, [Res("o")])
                    for c in range(8):
                        wg_t, wg_r = WB.get()
                        wgv = wg_t[:, :].rearrange("p (k x n) -> p k x n", k=8, x=3)
                        for x in range(3):
                            load_w(wgv[:, :, x, :], wG_d[l].rearrange("(k p) n -> p k n", p=128)[:, :, x * D + c * 128:x * D + (c + 1) * 128], (8, 128), wg_r)
                        wb_t, wb_r = WB.get()
                        wbv = wb_t[:, 0:1536].rearrange("p (x h n) -> p x h n", x=3, h=4)
                        for x, wd in enumerate((wba_d, wbb_d, wbc_d)):
                            load_w(wbv[:, x, :, :], wd[l].rearrange("(h p) n -> p h n", p=128)[:, :, c * 128:(c + 1) * 128], (4, 128), wb_r)
                        for (g0, w) in tls:
                            lo = g0 - t0
                            m_t, m_r = TMPF.get()
                            for x in range(3):
                                pg_t, pg_r = PS.get()
                                for k in range(8):
                                    mm(pg_t[:, 0:w], wgv[:, k, x, :], UB[:, 12 + k, lo:lo + w], [wg_r, R_XN], pg_r, start=(k == 0), stop=(k == 7))
                                pb_t, pb_r = PS.get()
                                for hh in range(4):
                                    mm(pb_t[:, 0:w], wbv[:, x, hh, :], UB[:, x * 4 + hh, lo:lo + w], [wb_r, otg_rs[x * 4 + hh]], pb_r, start=(hh == 0), stop=(hh == 3))
                                e_t, e_r = TMPF.get()
                                sigmoid_inplace(e_t[:, 0:w], pg_t[:, 0:w], [pg_r], e_r)
                                if x == 0:
                                    tt("dve", m_t[:, 0:w], e_t[:, 0:w], pb_t[:, 0:w], ALU.mult, [e_r, pb_r], [m_r])
                                else:
                                    tt("dve", e_t[:, 0:w], e_t[:, 0:w], pb_t[:, 0:w], ALU.mult, [e_r, pb_r], [e_r])
                                    tt("dve", m_t[:, 0:w], m_t[:, 0:w], e_t[:, 0:w], ALU.add, [m_r, e_r], [m_r])
                            cp("act", MT[:, c, lo:lo + w], m_t[:, 0:w], [m_r], [R_MT])
                    if DEBUG and l == 0:
                        dma("sp", lambda h: h.dma_start(out=dbg_m[:, :, t0:t0 + W], in_=MT[:, :, 0:W]), [R_MT], [Res("o")])
                    for c in range(8):
                        wo_t, wo_r = WB.get()
                        wov = wo_t[:, 0:1024].rearrange("p (k n) -> p k n", k=8)
                        load_w(wov, wo_d[l].rearrange("(k p) n -> p k n", p=128)[:, :, c * 128:(c + 1) * 128], (8, 128), wo_r)
                        for (g0, w) in tls:
                            lo = g0 - t0
                            py_t, py_r = PS.get()
                            for k in range(8):
                                mm(py_t[:, 0:w], wov[:, k, :], MT[:, k, lo:lo + w], [wo_r, R_MT], py_r, start=(k == 0), stop=(k == 7))
                            tt("dve", XT[:, c, lo:lo + w], XT[:, c, lo:lo + w], py_t[:, 0:w], ALU.add, [R_XT, py_r], [R_XT])
                    norm_half(XT, R_XT, W, tls, t0, 8, lambda c, lo, w: HT[:, c, lo:lo + w], R_HT)
                    cx.barrier()
                    for j in range(22):
                        wf_t, wf_r = WB.get()
                        wfv = wf_t[:, 0:2048].rearrange("p (g k n) -> p g k n", g=2, k=8)
                        load_w(wfv[:, 0, :, :], wfg_d[l].rearrange("(k p) n -> p k n", p=128)[:, :, j * 128:(j + 1) * 128], (8, 128), wf_r)
                        load_w(wfv[:, 1, :, :], wfu_d[l].rearrange("(k p) n -> p k n", p=128)[:, :, j * 128:(j + 1) * 128], (8, 128), wf_r)
                        for (g0, w) in tls:
                            lo = g0 - t0
                            pg_t, pg_r = PS.get()
                            pu_t, pu_r = PS.get()
                            for k in range(8):
                                mm(pg_t[:, 0:w], wfv[:, 0, k, :], HT[:, k, lo:lo + w], [wf_r, R_HT], pg_r, start=(k == 0), stop=(k == 7))
                            for k in range(8):
                                mm(pu_t[:, 0:w], wfv[:, 1, k, :], HT[:, k, lo:lo + w], [wf_r, R_HT], pu_r, start=(k == 0), stop=(k == 7))
                            e_t, e_r = TMPF.get()
                            sigmoid_inplace(e_t[:, 0:w], pg_t[:, 0:w], [pg_r], e_r)
                            tt("dve", e_t[:, 0:w], e_t[:, 0:w], pg_t[:, 0:w], ALU.mult, [e_r, pg_r], [e_r])
                            tt("dve", UB[:, j, lo:lo + w], e_t[:, 0:w], pu_t[:, 0:w], ALU.mult, [e_r, pu_r], [R_ACT])
                    for c in range(8):
                        wd_t, wd_r = WB.get()
                        wdv = wd_t[:, 0:2816].rearrange("p (j n) -> p j n", j=22)
                        load_w(wdv, wfd_d[l].rearrange("(j p) n -> p j n", p=128)[:, :, c * 128:(c + 1) * 128], (22, 128), wd_r)
                        for (g0, w) in tls:
                            lo = g0 - t0
                            py_t, py_r = PS.get()
                            for j in range(22):
                                mm(py_t[:, 0:w], wdv[:, j, :], UB[:, j, lo:lo + w], [wd_r, R_ACT], py_r, start=(j == 0), stop=(j == 21))
                            tt("dve", XT[:, c, lo:lo + w], XT[:, c, lo:lo + w], py_t[:, 0:w], ALU.add, [R_XT, py_r], [R_XT])
                    cx.barrier()
                    if (not last) or DEBUG:
                        dma("pool", lambda h: h.dma_start(out=xres.rearrange("(c p) t -> p c t", p=128)[:, :, t0:t0 + W], in_=XT[:, :, 0:W]), [R_XT], [R_xres])
                        norm_half(XT, R_XT, W, tls, t0, 16, lambda c, lo, w: UB[:, 12 + c, lo:lo + w], R_XN)
                        dma("pool", lambda h: h.dma_start(out=a1.rearrange("(c p) t -> p c t", p=128)[:, :, t0:t0 + W], in_=UB[:, 12:20, 0:W]), [R_XN], [R_a1])
                    else:
                        norm_half(XT, R_XT, W, tls, t0, 16, lambda c, lo, w: XT[:, c, lo:lo + w], R_XT)
                        for (g0, w) in tls:
                            for u in range(0, w, 128):
                                n = min(128, w - u)
                                lo = g0 - t0 + u
                                yl_t, yl_r = YL.get()
                                for cc in range(0, 8, 4):
                                    pt_, pr_ = PS.get()
                                    for c in range(cc, cc + 4):
                                        tp(pt_[0:n, (c - cc) * 128:(c - cc + 1) * 128], XT[:, c, lo:lo + n], 128, [R_XT], pr_)
                                    cp("act", yl_t[0:n, cc * 128:(cc + 4) * 128], pt_[0:n, :], [pr_], [yl_r])
                                dst = y_p[g0 + u:g0 + u + n, :] if g0 < 2048 else y_s[0:16, :]
                                dma("pool", lambda h: h.dma_start(out=dst, in_=yl_t[0:n, :]), [yl_r], [Res("o")])
                    cx.barrier()

        def load_w4(dst_ap, src_ap, shape, rdst):
            st_t, st_r = WST.get()
            a, b, c_ = shape
            view = st_t[:, 0:a * b * c_].rearrange("p (a b c) -> p a b c", a=a, b=b)
            dma("sp", lambda h: h.dma_start(out=view, in_=src_ap), [R_in], [st_r])
            cp("pool", dst_ap, view, [st_r], [rdst])

        load_layer_params(0)
        token_pre_layer0()
        for l in range(2):
            if STAGE < 2:
                break
            if l > 0:
                load_layer_params(l)
            if os.environ.get('MK_NOAG', '0') != '1':
                allgather(a1, b1, R_a1, R_b1, 8)
            if DEBUG and l == 0:
                with ExitStack() as dph:
                    DB, R_DB = single(cx, dph, "DB", [128, 4, NTOK], BF16)
                    dma("sp", lambda h: h.dma_start(out=DB[:, :, :], in_=b1[0].rearrange("(r p) t -> p r t", p=128)), [R_b1], [R_DB])
                    dma("sp", lambda h: h.dma_start(out=dbg_b1, in_=DB[:, :, :]), [R_DB], [Res("o")])
                    dma("sp", lambda h: h.dma_start(out=DB[:, 0, :], in_=a1[0:128, :]), [R_a1, R_DB], [R_DB])
                    dma("sp", lambda h: h.dma_start(out=dbg_a1, in_=DB[:, 0, :]), [R_DB], [Res("o")])
                    cx.barrier()
            if STAGE < 3:
                break
            head_phase(l)
            if STAGE < 4:
                break
            allgather(a2, b2, R_a2, R_b2, 12)
            token_post(l, last=(l == 1))
            if STAGE < 5:
                break
        cx.final_wait()
    return nc


_NC = None
_LAST = None


def _prep(inputs):
    f32 = np.float32
    g = lambda k: np.asarray(inputs[k])
    w_in = g("w_in")
    sp = [256, 512, 1024, 1040, 1552, 2064, 2576, 3088, 3092, 3604, 4116, 4628, 5140, 6164, 7188]
    consts = make_consts()
    shared = {
        "consts": consts,
        "wG": np.ascontiguousarray(w_in[:, :, 5140:8212]),
        "wba": g("w_branch_a"), "wbb": g("w_branch_b"), "wbc": g("w_branch_c"), "wo": g("w_out"),
        "n1": g("norm1_g"), "n2": g("norm2_g"), "nf": g("final_norm_g"),
        "wfg": g("w_ffn_gate"), "wfu": g("w_ffn_up"), "wfd": g("w_ffn_down"),
        "gng": g("gla_norm_g"), "hng": g("hg_norm_g"),
    }
    ck, cv, clf = g("cache_fox_k"), g("cache_fox_v"), g("cache_fox_logf")
    per_head = []
    for h in range(4):
        gq = w_in[:, :, 0 + 64 * h:64 * h + 64]
        gk = w_in[:, :, 256 + 64 * h:256 + 64 * h + 64]
        gv = w_in[:, :, 512 + 128 * h:512 + 128 * h + 128]
        glr = w_in[:, :, 1024:1040]
        gr = w_in[:, :, 1040 + 128 * h:1040 + 128 * h + 128]
        fq = w_in[:, :, 1552 + 128 * h:1552 + 128 * h + 128]
        fk = w_in[:, :, 2064 + 128 * h:2064 + 128 * h + 128]
        fv = w_in[:, :, 2576 + 128 * h:2576 + 128 * h + 128]
        ff = w_in[:, :, 3088 + h:3089 + h]
        hq = w_in[:, :, 3092 + 128 * h:3092 + 128 * h + 128]
        hf = w_in[:, :, 3604 + 128 * h:3604 + 128 * h + 128]
        hi = w_in[:, :, 4116 + 128 * h:4116 + 128 * h + 128]
        hgg = w_in[:, :, 4628 + 128 * h:4628 + 128 * h + 128]
        wF = np.ascontiguousarray(np.concatenate([gq, gk, gr, glr, fq, fk, hq, hf, hgg], axis=2))
        wT = np.ascontiguousarray(np.concatenate([gk, gv, fk, fv, ff, hf, hi], axis=2))
        d = {
            "wF": wF, "wT": wT,
            "wg2": np.ascontiguousarray(g("gla_wg2")[:, :, 64 * h:64 * h + 64]),
            "bg": np.ascontiguousarray(g("gla_bg")[:, 64 * h:64 * h + 64]),
            "fbf": np.ascontiguousarray(g("fox_bf")[:, h:h + 1]),
            "hlb": np.ascontiguousarray(g("hg_lb_logits")[:, 128 * h:128 * h + 128]),
            "ck": np.ascontiguousarray(ck[:, :, :, h, :]).reshape(2 * NPOOL * 32, 512),
            "cv": np.ascontiguousarray(cv[:, :, :, h, :]).reshape(2 * NPOOL * 32, 512),
            "clf": np.ascontiguousarray(clf[:, :, :, h]).reshape(2 * NPOOL * 32, 4),
        }
        per_head.append(d)
    in_maps = []
    p = np.arange(128)
    for c in range(8):
        b, h = c // 4, c % 4
        m = dict(shared)
        m.update(per_head[h])
        m["xp"] = np.ascontiguousarray(g("x_prompt")[b, 2048 * h:2048 * h + 2048])
        m["xs"] = np.ascontiguousarray(g("x_sample")[16 * b + 4 * h:16 * b + 4 * h + 4].reshape(16, D))
        m["sgla"] = np.ascontiguousarray(g("state_gla")[:, 16 * b:16 * b + 16, h])
        m["shg"] = np.ascontiguousarray(g("state_hgrn")[:, 16 * b:16 * b + 16, h])
        m["pt"] = np.ascontiguousarray(g("page_table")[16 * b:16 * b + 16]).astype(np.int32)
        ridx = np.zeros((128, 24), np.int32)
        for hv in range(2):
            for s_ in range(4):
                for x in range(3):
                    ridx[:, hv * 12 + s_ * 3 + x] = 2 * ((h * 3 + x) * 512 + s_ * 128 + p) + hv
        m["ridx"] = ridx
        in_maps.append(m)
    return in_maps


def kernel(**inputs):
    global _NC
    if _NC is None:
        _NC = build_program()
    in_maps = _prep(inputs)
    res = run_bass_kernel_spmd(_NC, in_maps, core_ids=list(range(8)))
    R = res.results
    global _LAST
    _LAST = R
    f32 = np.float32
    y_p = np.zeros((2, SEQ, D), f32)
    y_s = np.zeros((32, 4, D), f32)
    gla_p = np.zeros((2, 2, 4, 64, 128), f32)
    gla_s = np.zeros((2, 32, 4, 64, 128), f32)
    fk_p = np.zeros((2, 2, SEQ, 4, 128), f32)
    fv_p = np.zeros((2, 2, SEQ, 4, 128), f32)
    lf_p = np.zeros((2, 2, SEQ, 4), f32)
    fk_s = np.zeros((2, 32, 4, 4, 128), f32)
    fv_s = np.zeros((2, 32, 4, 4, 128), f32)
    lf_s = np.zeros((2, 32, 4, 4), f32)
    hg_p = np.zeros((2, 2, 4, 128, 128), f32)
    hg_s = np.zeros((2, 32, 4, 128, 128), f32)
    for c in range(8):
        b, h = c // 4, c % 4
        r = R[c]
        y_p[b, 2048 * h:2048 * h + 2048] = r["y_p"]
        y_s[16 * b + 4 * h:16 * b + 4 * h + 4] = r["y_s"].reshape(4, 4, D)
        gla_p[:, b, h] = r["gla_p"]
        gla_s[:, 16 * b:16 * b + 16, h] = r["gla_s"]
        fk_p[:, b, :, h, :] = r["fk_p"]
        fv_p[:, b, :, h, :] = r["fv_p"]
        lf_p[:, b, :, h] = r["lf_p"].reshape(2, SEQ)
        fk_s[:, 16 * b:16 * b + 16, :, h, :] = r["fk_s"]
        fv_s[:, 16 * b:16 * b + 16, :, h, :] = r["fv_s"]
        lf_s[:, 16 * b:16 * b + 16, :, h] = r["lf_s"]
        hg_p[:, b, h] = r["hg_p"]
        hg_s[:, 16 * b:16 * b + 16, h] = r["hg_s"]
    return (y_p, y_s, gla_p, gla_s, fk_p, fv_p, lf_p, fk_s, fv_s, lf_s, hg_p, hg_s)
```
